# Optimizing a Trainium2 kernel written in Bass

```python
import jax, jax.numpy as jnp
from jax import lax
import numpy as np

D_MODEL = 1024
BATCH = 8
SEQ = 4096
DEPTH = 2

CHUNK = 64
N_A = DEPTH // 2
N_B = DEPTH - N_A
CONV_WIDTH = 3
N_HEADS = 16
HEAD_DIM = D_MODEL // N_HEADS
D_FF = -(-8 * D_MODEL // (3 * 256)) * 256
Q_BLOCK = 128
N_MOD = 6
LN_EPS = 1e-5
DEEPNORM_ALPHA = (2.0 * DEPTH) ** 0.25
DEEPNORM_BETA = (8.0 * DEPTH) ** -0.25

kernel_name = "yoco_shortconv_fox_deepnorm_adaln"


def layer_norm(x, g, b):
    xf = x.astype(jnp.float32)
    mu = jnp.mean(xf, axis=-1, keepdims=True)
    var = jnp.mean(jnp.square(xf - mu), axis=-1, keepdims=True)
    y = (xf - mu) * lax.rsqrt(var + LN_EPS) * g + b
    return y.astype(x.dtype)


def modulate(x, shift, scale):
    return x * (1.0 + scale[:, None, :]) + shift[:, None, :]


def short_conv_mixer(h, w_in, conv_w, w_out):
    d = h.shape[-1]
    gb, gc, v = jnp.split(h @ w_in, 3, axis=-1)
    u = gc * v
    conv = lax.conv_general_dilated(
        u, conv_w[:, None, :].astype(u.dtype),
        window_strides=(1,), padding=[(CONV_WIDTH - 1, 0)],
        dimension_numbers=("NWC", "WIO", "NWC"), feature_group_count=d)
    return (gb * conv) @ w_out


def shared_kv(x, c, w_ada_kv, b_ada_kv, w_k, w_v, w_f, b_f):
    bsz, s, _ = x.shape
    kv_shift, kv_scale = jnp.split(jax.nn.silu(c) @ w_ada_kv + b_ada_kv, 2, axis=-1)
    h = modulate(x, kv_shift, kv_scale)
    k = (h @ w_k).reshape(bsz, s, N_HEADS, HEAD_DIM).transpose(0, 2, 1, 3)
    v = (h @ w_v).reshape(bsz, s, N_HEADS, HEAD_DIM).transpose(0, 2, 1, 3)
    log_f = jax.nn.log_sigmoid((h @ w_f + b_f).astype(jnp.float32))
    log_fcum = jnp.cumsum(log_f, axis=1).transpose(0, 2, 1)
    return k, v, log_fcum


def forgetting_attention(h, w_q, k, v, log_fcum, w_o):
    bsz, s, d = h.shape
    q = (h @ w_q).reshape(bsz, s, N_HEADS, HEAD_DIM).transpose(0, 2, 1, 3) * (HEAD_DIM ** -0.5)
    outs = []
    for i in range(s // Q_BLOCK):
        q0 = i * Q_BLOCK
        kend = q0 + Q_BLOCK
        qb = q[:, :, q0:kend]
        kb = k[:, :, :kend]
        vb = v[:, :, :kend]
        logits = jnp.einsum("bhqd,bhkd->bhqk", qb, kb, preferred_element_type=jnp.float32)
        logits = logits + log_fcum[:, :, q0:kend, None] - log_fcum[:, :, None, :kend]
        qpos = q0 + jnp.arange(Q_BLOCK)
        kpos = jnp.arange(kend)
        logits = jnp.where(kpos[None, :] <= qpos[:, None], logits, -jnp.inf)
        p = jax.nn.softmax(logits, axis=-1)
        outs.append(jnp.einsum("bhqk,bhkd->bhqd", p.astype(vb.dtype), vb))
    o = jnp.concatenate(outs, axis=2).transpose(0, 2, 1, 3).reshape(bsz, s, d)
    return o @ w_o


def swiglu(h, w_gate, w_up, w_down):
    return (jax.nn.silu(h @ w_gate) * (h @ w_up)) @ w_down


def setup_inputs(seed: int = 0) -> dict:
    key = jax.random.key(seed)
    ks = jax.random.split(key, 24)
    D, F, H = D_MODEL, D_FF, N_HEADS
    nrm = lambda k, shape, scale: jax.random.normal(k, shape, jnp.float32) * scale
    return {
        "x": nrm(ks[0], (BATCH, SEQ, D), 1.0),
        "c": nrm(ks[1], (BATCH, D), 1.0),
        "w_ada": nrm(ks[2], (DEPTH, D, N_MOD * D), 0.5 * D ** -0.5),
        "b_ada": nrm(ks[3], (DEPTH, N_MOD * D), 0.02),
        "conv_w_in": nrm(ks[4], (N_A, D, 3 * D), D ** -0.5),
        "conv_w": nrm(ks[5], (N_A, CONV_WIDTH, D), CONV_WIDTH ** -0.5),
        "conv_w_out": nrm(ks[6], (N_A, D, D), DEEPNORM_BETA * D ** -0.5),
        "w_ada_kv": nrm(ks[7], (D, 2 * D), 0.5 * D ** -0.5),
        "b_ada_kv": nrm(ks[8], (2 * D,), 0.02),
        "w_k": nrm(ks[9], (D, D), D ** -0.5),
        "w_v": nrm(ks[10], (D, D), DEEPNORM_BETA * D ** -0.5),
        "w_f": nrm(ks[11], (D, H), D ** -0.5),
        "b_f": jax.random.uniform(ks[12], (H,), jnp.float32, 1.0, 5.0),
        "attn_w_q": nrm(ks[13], (N_B, D, D), D ** -0.5),
        "attn_w_o": nrm(ks[14], (N_B, D, D), DEEPNORM_BETA * D ** -0.5),
        "ffn_w_gate": nrm(ks[15], (DEPTH, D, F), D ** -0.5),
        "ffn_w_up": nrm(ks[16], (DEPTH, D, F), D ** -0.5),
        "ffn_w_down": nrm(ks[17], (DEPTH, F, D), DEEPNORM_BETA * F ** -0.5),
        "ln1_g": 1.0 + nrm(ks[18], (DEPTH, D), 0.02),
        "ln1_b": nrm(ks[19], (DEPTH, D), 0.02),
        "ln2_g": 1.0 + nrm(ks[20], (DEPTH, D), 0.02),
        "ln2_b": nrm(ks[21], (DEPTH, D), 0.02),
    }


def reference(x, c, w_ada, b_ada, conv_w_in, conv_w, conv_w_out, w_ada_kv, b_ada_kv,
              w_k, w_v, w_f, b_f, attn_w_q, attn_w_o, ffn_w_gate, ffn_w_up, ffn_w_down,
              ln1_g, ln1_b, ln2_g, ln2_b):
    c_act = jax.nn.silu(c)
    k = v = log_fcum = None
    for l in range(DEPTH):
        sh1, sc1, g1, sh2, sc2, g2 = jnp.split(c_act @ w_ada[l] + b_ada[l], N_MOD, axis=-1)
        h = modulate(x, sh1, sc1)
        if l < N_A:
            y = short_conv_mixer(h, conv_w_in[l], conv_w[l], conv_w_out[l])
        else:
            if l == N_A:
                k, v, log_fcum = shared_kv(x, c, w_ada_kv, b_ada_kv, w_k, w_v, w_f, b_f)
            j = l - N_A
            y = forgetting_attention(h, attn_w_q[j], k, v, log_fcum, attn_w_o[j])
        x = layer_norm(DEEPNORM_ALPHA * x + g1[:, None, :] * y, ln1_g[l], ln1_b[l])
        h = modulate(x, sh2, sc2)
        y = swiglu(h, ffn_w_gate[l], ffn_w_up[l], ffn_w_down[l])
        x = layer_norm(DEEPNORM_ALPHA * x + g2[:, None, :] * y, ln2_g[l], ln2_b[l])
    return x
```

```python
import numpy as np
from contextlib import ExitStack
import concourse.bass as bass
import concourse.mybir as mybir
from concourse.bass_utils import run_bass_kernel_spmd

F32 = mybir.dt.float32
BF16 = mybir.dt.bfloat16
AF = mybir.ActivationFunctionType
ALU = mybir.AluOpType

D = 1024
S = 4096
NH = 16
DFF = 2816
NFC = DFF // 128
TT = 512
NT = S // TT
NTR = NT
SKIP = set()
KC = D // 128
ALPHA = float((2.0 * 2) ** 0.25)
EPS = 1e-5
NEG = -30000.0


class Sm:
    def __init__(self, nc, name):
        self.h = nc.alloc_semaphore(name)
        self.n = 0
        self.tag = None


class Eng:
    def __init__(self, nc, e, name):
        self.e = e
        self.sem = Sm(nc, "es_" + name)
        self.waited = {}
        self.name = name

    def wait(self, sm, val):
        if val <= 0 or self.waited.get(sm, 0) >= val:
            return
        self.e.wait_ge(sm.h, val)
        self.waited[sm] = val


class Res:
    __slots__ = ("w", "rs")

    def __init__(self):
        self.w = {}
        self.rs = {}


def build(upto="all"):
    nc = bass.Bass("TRN2", target_bir_lowering=False)

    def din(name, shape, dt=F32):
        return nc.dram_tensor(name, shape, dt, kind="ExternalInput").ap()

    x = din("x", [S, D])
    cT = din("cT", [128, 8])
    w_ada = din("w_ada", [2, D, 6 * D])
    b_all = din("b_all", [1, 14336])
    w_ada_kv = din("w_ada_kv", [D, 2 * D])
    conv_w_in = din("conv_w_in", [D, 3 * D])
    convw = din("convw", [128, 24])
    conv_w_out = din("conv_w_out", [D, D])
    w_k = din("w_k", [D, D])
    w_v = din("w_v", [D, D])
    w_f = din("w_f", [D, NH])
    b_f = din("b_f", [NH, 1])
    w_q = din("w_q", [D, D])
    w_o = din("w_o", [D, D])
    w_gate = din("w_gate", [2, D, DFF])
    w_up = din("w_up", [2, D, DFF])
    w_down = din("w_down", [2, DFF, D])
    lnv = din("lnv", [128, 64])
    ident_in = din("ident", [128, 128])
    mask_in = din("trimask", [128, 512])
    out = nc.dram_tensor("out", [S, D], F32, kind="ExternalOutput").ap()

    modscr = nc.dram_tensor("modscr", [14336], F32).ap()
    AXS = [nc.dram_tensor("axs%d" % i, [NT, 128, KC * TT], F32).ap() for i in range(3)]
    HS = [nc.dram_tensor("hs%d" % i, [NT, 128, KC * TT], BF16).ap() for i in range(2)]
    KSd = nc.dram_tensor("ksd", [NH, 64, S], BF16).ap()
    VSd = nc.dram_tensor("vsd", [NH, 128, S // 128, 128], BF16).ap()
    AXSres = [[Res() for _ in range(NT)] for _ in range(3)]
    HSres = [[Res() for _ in range(NT)] for _ in range(2)]
    KSres = [Res() for _ in range(NT)]
    VSres = [Res() for _ in range(NT)]

    PE = Eng(nc, nc.tensor, "pe")
    ACT = Eng(nc, nc.scalar, "act")
    DVE = Eng(nc, nc.vector, "dve")
    POOL = Eng(nc, nc.gpsimd, "pool")
    SP = Eng(nc, nc.sync, "sp")

    def op(eng, fn, reads=(), writes=()):
        own = eng.sem
        for r in reads:
            for sm, v in r.w.items():
                if not (eng is PE and sm is own):
                    eng.wait(sm, v)
        for r in writes:
            for sm, v in r.w.items():
                if not (eng is PE and sm is own):
                    eng.wait(sm, v)
            for sm, v in r.rs.items():
                if not (eng is PE and sm is own):
                    eng.wait(sm, v)
        ins = fn()
        own.n += 1
        ins.then_inc(own.h, 1)
        for r in reads:
            r.rs[own] = own.n
        for r in writes:
            r.w[own] = own.n
            r.rs = {}
        return ins

    def pe_group(fns, reads, writes):
        def run():
            ins = None
            for f in fns:
                ins = f()
            return ins
        return op(PE, run, reads, writes)

    dma_sems = {}

    def dsem(name):
        if name not in dma_sems:
            dma_sems[name] = Sm(nc, "dq_" + name)
        return dma_sems[name]

    def dma(semname, tag, out_ap, in_ap, reads=(), writes=(), q=None, **kw):
        q = q or SP
        sm = dsem(semname)
        for r in reads:
            for s_, v in r.w.items():
                q.wait(s_, v)
        for r in writes:
            for s_, v in r.w.items():
                q.wait(s_, v)
            for s_, v in r.rs.items():
                q.wait(s_, v)
        if sm.tag is not None and sm.tag != tag:
            q.wait(sm, sm.n)
        sm.tag = tag
        pairs = out_ap if isinstance(out_ap, list) else [(out_ap, in_ap)]
        for o_, i_ in pairs:
            ins = q.e.dma_start(out=o_, in_=i_, **kw)
            ins.then_inc(sm.h, 16)
            sm.n += 16
        for r in reads:
            r.rs[sm] = sm.n
        for r in writes:
            r.w[sm] = sm.n
            r.rs = {}

    uid = [0]

    def SBT(name, shape, dt):
        uid[0] += 1
        return nc.sbuf_tensor("%s_%d" % (name, uid[0]), shape, dt)

    def barrier():
        engs = [PE, ACT, DVE, POOL, SP]
        sems = [e.sem for e in engs if e is not SP] + list(dma_sems.values())
        for e in engs:
            for sm in sems:
                if sm is not e.sem or e is not PE:
                    e.wait(sm, sm.n)

    banks = [nc.alloc_psum_tensor("bank%d" % i, [128, 512], F32) for i in range(8)]
    bank_res = [Res() for _ in range(8)]
    ring = {"i": 0}

    held = set()

    def nbank(hold=False):
        i = ring["i"]
        while i in held:
            i = (i + 1) % 8
        ring["i"] = (i + 1) % 8
        if hold:
            held.add(i)
        return banks[i], bank_res[i]

    def release(bank):
        held.discard(banks.index(bank))

    A = nc.alloc_sbuf_tensor
    ident = A("ident_sb", [128, 128], F32)
    identb = A("identb", [128, 128], BF16)
    onesb = A("onesb", [128, 128], BF16)
    maskf = A("maskf", [128, 512], F32)
    maskb = A("maskb", [128, 512], BF16)
    modT = A("modT", [128, 112], F32)
    lnvt = A("lnvt", [128, 64], F32)
    cvw = A("cvw", [128, 24], F32)
    cin = A("cin", [128, 8], F32)
    cact = A("cact", [128, 8], F32)
    nbf = A("nbf", [NH, 1], F32)
    NV = 16
    pv = A("pv", [128, NV * 8], F32)
    Rconst = Res()
    Rmod = Res()
    Rpv = Res()
    op(DVE, lambda: nc.vector.memset(pv[:], 0.0), writes=[Rpv])

    dma("small", "c0", ident[:], ident_in, writes=[Rconst])
    dma("small", "c0", maskf[:], mask_in, writes=[Rconst])
    dma("small", "c0", lnvt[:], lnv, writes=[Rconst])
    dma("small", "c0", cvw[:], convw, writes=[Rconst])
    dma("small", "c0", cin[:], cT, writes=[Rconst])
    dma("small", "c0", nbf[:], b_f, writes=[Rconst])
    Rc2 = Res()
    op(DVE, lambda: nc.vector.tensor_copy(out=identb[:], in_=ident[:]), reads=[Rconst], writes=[Rc2])
    op(DVE, lambda: nc.vector.tensor_copy(out=maskb[:], in_=maskf[:]), reads=[Rconst], writes=[Rc2])
    op(DVE, lambda: nc.vector.memset(onesb[:], 1.0), writes=[Rc2])
    epsc = A("epsc", [128, 1], F32)
    op(DVE, lambda: nc.vector.memset(epsc[:], EPS), writes=[Rc2])
    op(DVE, lambda: nc.vector.tensor_scalar(out=nbf[:], in0=nbf[:], scalar1=-1.0, scalar2=None, op0=ALU.mult),
       reads=[Rconst], writes=[Rc2])
    op(ACT, lambda: nc.scalar.activation(out=cact[:], in_=cin[:], func=AF.Silu), reads=[Rconst], writes=[Rc2])

    def mvec(l, v):
        return modT[:, l * 48 + v * 8: l * 48 + v * 8 + 8]

    def kvvec(v):
        return modT[:, 96 + v * 8: 96 + v * 8 + 8]

    def lvec(v, l):
        return lnvt[:, (v * 2 + l) * 8: (v * 2 + l) * 8 + 8]

    def pvec(i):
        return pv[:, i * 8: i * 8 + 8]

    def stage_mod():
        with ExitStack() as es:
            stg = [es.enter_context(SBT("mstg%d" % i, [128, 1024], F32)) for i in range(6)]
            stg_res = [Res() for _ in range(6)]
            brow = es.enter_context(SBT("brow", [1, 14336], F32))
            mrow = es.enter_context(SBT("mrow", [1, 14336], F32))
            Rb = Res()
            Rm = Res()
            dma("small", "brow", brow[:], b_all, writes=[Rb])
            cnt = 0
            for g in range(7):
                bk = [nbank() for _ in range(4)]
                for kc in range(KC):
                    for half in range(2):
                        si = cnt % 6
                        cnt += 1
                        if g < 6:
                            c_lo = (g % 3) * 2048 + half * 1024
                            src = w_ada[g // 3, kc * 128:(kc + 1) * 128, c_lo:c_lo + 1024]
                        else:
                            src = w_ada_kv[kc * 128:(kc + 1) * 128, half * 1024:(half + 1) * 1024]
                        dma("wstg%d" % si, "m%d" % cnt, stg[si][:], src, writes=[stg_res[si]],
                            q=(SP if cnt % 2 == 0 else POOL))
                        fns = []
                        for b2 in range(2):
                            b = half * 2 + b2
                            fns.append(lambda b=b, b2=b2, kc=kc, si=si: nc.tensor.matmul(
                                bk[b][0][0:1, :], cact[:, kc:kc + 1], stg[si][:, b2 * 512:(b2 + 1) * 512],
                                start=(kc == 0), stop=(kc == KC - 1)))
                        pe_group(fns, reads=[stg_res[si], Rc2], writes=[bk[half * 2][1], bk[half * 2 + 1][1]])
                for b in range(4):
                    c0 = g * 2048 + b * 512
                    op(DVE, lambda b=b, c0=c0: nc.vector.tensor_tensor(
                        out=mrow[0:1, c0:c0 + 512], in0=bk[b][0][0:1, :], in1=brow[0:1, c0:c0 + 512], op=ALU.add),
                       reads=[bk[b][1], Rb], writes=[Rm])
            Rscr = Res()
            dma("small", "mscr", modscr.rearrange("(o n) -> o n", o=1), mrow[:], reads=[Rm], writes=[Rscr])
            dma("small", "mscr2", modT[:], modscr.rearrange("(c p) -> p c", p=128), reads=[Rscr], writes=[Rmod],
                allow_slow_non_contiguous=True)
        V = nc.vector
        def dv(fn):
            op(DVE, fn, reads=[Rmod, Rconst, Rpv], writes=[Rpv])
        dv(lambda: V.tensor_scalar(out=pvec(0), in0=mvec(0, 1), scalar1=1.0, scalar2=None, op0=ALU.add))
        for l in range(2):
            b = 1 + l * 5
            dv(lambda l=l, b=b: V.tensor_scalar(out=pvec(b), in0=lvec(0, l), scalar1=ALPHA, scalar2=None, op0=ALU.mult))
            dv(lambda l=l, b=b: V.tensor_scalar(out=pvec(b + 1), in0=lvec(1, l), scalar1=ALPHA, scalar2=None, op0=ALU.mult))
            dv(lambda l=l, b=b: V.tensor_scalar(out=pvec(b + 4), in0=mvec(l, 4), scalar1=1.0, scalar2=None, op0=ALU.add))
            dv(lambda l=l, b=b: V.tensor_tensor(out=pvec(b + 2), in0=lvec(0, l), in1=pvec(b + 4), op=ALU.mult))
            dv(lambda l=l, b=b: V.tensor_tensor(out=pvec(b + 3), in0=lvec(1, l), in1=pvec(b + 4), op=ALU.mult))
            dv(lambda l=l, b=b: V.tensor_tensor(out=pvec(b + 3), in0=pvec(b + 3), in1=mvec(l, 3), op=ALU.add))
        dv(lambda: V.tensor_scalar(out=pvec(11), in0=lvec(2, 0), scalar1=ALPHA, scalar2=None, op0=ALU.mult))
        dv(lambda: V.tensor_scalar(out=pvec(12), in0=lvec(3, 0), scalar1=ALPHA, scalar2=None, op0=ALU.mult))
        dv(lambda: V.tensor_scalar(out=pvec(13), in0=mvec(1, 1), scalar1=1.0, scalar2=1.0 / ALPHA, op0=ALU.add, op1=ALU.mult))
        dv(lambda: V.tensor_scalar(out=pvec(14), in0=kvvec(1), scalar1=1.0, scalar2=1.0 / ALPHA, op0=ALU.add, op1=ALU.mult))

    RV = [Rpv, Rmod, Rconst, Rc2]

    def load_w(es_stg, dst, dst_res, src, K, N, state):
        stg, stg_res = es_stg
        ns = len(stg)
        for kc in range(K // 128):
            eng = [DVE, ACT][state[0] % 2]
            state[0] += 1
            for n0 in range(0, N, 1024):
                n1 = min(N, n0 + 1024)
                si = state[1] % ns
                state[1] += 1
                dma("wstg%d" % si, "w%d" % state[1], stg[si][:, 0:n1 - n0], src[kc * 128:(kc + 1) * 128, n0:n1],
                    writes=[stg_res[si]], q=(SP if state[1] % 2 == 0 else POOL))
                if eng is DVE:
                    op(DVE, lambda si=si, kc=kc, n0=n0, n1=n1: nc.vector.tensor_copy(out=dst[:, kc, n0:n1], in_=stg[si][:, 0:n1 - n0]),
                       reads=[stg_res[si]], writes=[dst_res[kc]])
                else:
                    op(ACT, lambda si=si, kc=kc, n0=n0, n1=n1: nc.scalar.activation(out=dst[:, kc, n0:n1], in_=stg[si][:, 0:n1 - n0],
                                                                                   func=AF.Copy),
                       reads=[stg_res[si]], writes=[dst_res[kc]])

    wstate = [0, 0]

    def ln_steps(AX, AXres, RB, RBres, RS, RSres, aG, aB, hA=None, hB=None, H=None, Hres=None, dve_out=False):
        V = nc.vector
        st = {}
        steps = []

        def s_stats1():
            st["mb"], st["mr"] = nbank(hold=True)
            mb = st["mb"]
            pe_group([lambda m=m: nc.tensor.matmul(mb[:, :], onesb[:, :], RB[:, m, :], start=(m == 0), stop=(m == KC - 1))
                      for m in range(KC)], reads=[RBres[m] for m in range(KC)] + [Rc2], writes=[st["mr"]])
        steps.append(s_stats1)

        def s_center(m):
            mb, mr = st["mb"], st["mr"]
            op(DVE, lambda: V.scalar_tensor_tensor(out=AX[:, m, :], in0=mb[:, :], scalar=-1.0 / D, in1=AX[:, m, :],
                                                   op0=ALU.mult, op1=ALU.add), reads=[mr], writes=[AXres[m]])
            op(DVE, lambda: V.tensor_tensor(out=RB[:, m, :], in0=AX[:, m, :], in1=AX[:, m, :], op=ALU.mult),
               reads=[AXres[m]], writes=[RBres[m]])
            if m == KC - 1:
                release(mb)
        for m in range(KC):
            steps.append(lambda m=m: s_center(m))

        def s_stats2():
            vb, vr = nbank()
            pe_group([lambda m=m: nc.tensor.matmul(vb[:, :], onesb[:, :], RB[:, m, :], start=(m == 0), stop=(m == KC - 1))
                      for m in range(KC)], reads=[RBres[m] for m in range(KC)] + [Rc2], writes=[vr])
            op(ACT, lambda: nc.scalar.activation(out=RS[:, :], in_=vb[:, :], func=AF.Ln, scale=1.0 / D, bias=epsc[:, 0:1]),
               reads=[vr, Rc2], writes=[RSres])
            op(ACT, lambda: nc.scalar.activation(out=RS[:, :], in_=RS[:, :], func=AF.Exp, scale=-0.5),
               reads=[RSres], writes=[RSres])
        steps.append(s_stats2)

        def s_out(m):
            op(DVE, lambda: V.tensor_tensor(out=AX[:, m, :], in0=AX[:, m, :], in1=RS[:, :], op=ALU.mult),
               reads=[RSres], writes=[AXres[m]])
            if H is not None:
                if dve_out:
                    op(DVE, lambda: V.tensor_scalar(out=H[:, m, :], in0=AX[:, m, :], scalar1=hA[:, m:m + 1],
                                                    scalar2=hB[:, m:m + 1], op0=ALU.mult, op1=ALU.add),
                       reads=[AXres[m]] + RV, writes=[Hres[m]])
                else:
                    op(ACT, lambda: nc.scalar.activation(out=H[:, m, :], in_=AX[:, m, :], func=AF.Identity,
                                                         scale=hA[:, m:m + 1], bias=hB[:, m:m + 1]),
                       reads=[AXres[m]] + RV, writes=[Hres[m]])
            if dve_out:
                op(DVE, lambda: V.tensor_scalar(out=AX[:, m, :], in0=AX[:, m, :], scalar1=aG[:, m:m + 1],
                                                scalar2=aB[:, m:m + 1], op0=ALU.mult, op1=ALU.add),
                   reads=RV, writes=[AXres[m]])
            else:
                op(ACT, lambda: nc.scalar.activation(out=AX[:, m, :], in_=AX[:, m, :], func=AF.Identity,
                                                     scale=aG[:, m:m + 1], bias=aB[:, m:m + 1]),
                   reads=RV, writes=[AXres[m]])
        for m in range(KC):
            steps.append(lambda m=m: s_out(m))
        return steps

    def finish_ln(*a, **kw):
        for f in ln_steps(*a, **kw):
            f()

    def resid(AX, AXres, RB, RBres, m, bank, bres, gvec, rb_dve=False):
        op(DVE, lambda: nc.vector.scalar_tensor_tensor(out=AX[:, m, :], in0=bank[:, :], scalar=gvec[:, m:m + 1],
                                                       in1=AX[:, m, :], op0=ALU.mult, op1=ALU.add),
           reads=[bres] + RV, writes=[AXres[m]])
        if rb_dve:
            op(DVE, lambda: nc.vector.tensor_copy(out=RB[:, m, :], in_=AX[:, m, :]), reads=[AXres[m]], writes=[RBres[m]])
        else:
            op(ACT, lambda: nc.scalar.activation(out=RB[:, m, :], in_=AX[:, m, :], func=AF.Copy),
               reads=[AXres[m]], writes=[RBres[m]])

    def flat(t):
        return t[:, :, :].rearrange("p a b -> p (a b)")

    def interleave(main, bg, start=1, every=1):
        for i, f in enumerate(main):
            f()
            if i >= start and bg and (i - start) % every == 0:
                bg.pop(0)()

    def stage_conv():
        with ExitStack() as es:
            Win = es.enter_context(SBT("Win", [128, KC, 3 * D], BF16))
            Wout = es.enter_context(SBT("Wout", [128, KC, D], BF16))
            Winr = [Res() for _ in range(KC)]
            Woutr = [Res() for _ in range(KC)]
            with ExitStack() as es2:
                stg = [es2.enter_context(SBT("wstg%d" % i, [128, 1024], F32)) for i in range(6)]
                sres = [Res() for _ in range(6)]
                load_w((stg, sres), Win, Winr, conv_w_in, D, 3 * D, wstate)
                load_w((stg, sres), Wout, Woutr, conv_w_out, D, D, wstate)
                barrier()
            XT = [es.enter_context(SBT("XT%d" % i, [128, 4, D], F32)) for i in range(2)]
            XTr = [Res() for _ in range(2)]
            AXs = [es.enter_context(SBT("AX%d" % i, [128, KC, TT], F32)) for i in range(2)]
            AXr = [[Res() for _ in range(KC)] for _ in range(2)]
            Hs = [es.enter_context(SBT("H%d" % i, [128, KC, TT], BF16)) for i in range(2)]
            Hr = [[Res() for _ in range(KC)] for _ in range(2)]
            H2 = es.enter_context(SBT("H2", [128, KC, TT], BF16))
            H2r = [Res() for _ in range(KC)]
            Z = es.enter_context(SBT("Z", [128, KC, TT], BF16))
            Zr = [Res() for _ in range(KC)]
            RB = es.enter_context(SBT("RB", [128, KC, TT], BF16))
            RBr = [Res() for _ in range(KC)]
            U = es.enter_context(SBT("U", [128, KC, TT + 2], F32))
            Ur = [Res() for _ in range(KC)]
            VSb = [es.enter_context(SBT("VSb%d" % i, [128, TT], F32)) for i in range(2)]
            VSr = [Res() for _ in range(2)]
            CV = [es.enter_context(SBT("CV%d" % i, [128, TT], F32)) for i in range(2)]
            CVr = [Res() for _ in range(2)]
            RS = es.enter_context(SBT("RS", [128, TT], F32))
            RSr = Res()
            T1 = es.enter_context(SBT("T1", [128, TT], F32))
            T1r = Res()
            for m in range(KC):
                if "umem" in SKIP:
                    break
                op(POOL, lambda m=m: nc.gpsimd.memset(U[:, m, 0:2], 0.0), writes=[Ur[m]])

            def load_x(t):
                dma("xin%d" % (t % 2), "x%d" % t, XT[t % 2][:, :, :],
                    x[t * TT:(t + 1) * TT, :].rearrange("(s p) d -> p s d", p=128), writes=[XTr[t % 2]])
            if "loadx" not in SKIP:
                load_x(0)
            A0 = pvec(0)
            B0 = mvec(0, 0)
            g1 = mvec(0, 2)
            vstate = [0]

            def front_steps(t):
                xt = XT[t % 2]
                ax, axr = AXs[t % 2], AXr[t % 2]
                h, hr = Hs[t % 2], Hr[t % 2]
                steps = []

                def sA(ch):
                    if ch == 0 and t + 1 < NTR:
                        load_x(t + 1)
                    bk, br = nbank()
                    pe_group([lambda s_=s_: nc.tensor.transpose(bk[:, s_ * 128:(s_ + 1) * 128],
                                                                xt[:, s_, ch * 128:(ch + 1) * 128], ident[:, :])
                              for s_ in range(4)], reads=[XTr[t % 2], Rconst], writes=[br])
                    op(ACT, lambda: nc.scalar.activation(out=ax[:, ch, :], in_=bk[:, :], func=AF.Copy, scale=ALPHA),
                       reads=[br], writes=[axr[ch]])
                    op(ACT, lambda: nc.scalar.activation(out=h[:, ch, :], in_=bk[:, :], func=AF.Identity,
                                                         scale=A0[:, ch:ch + 1], bias=B0[:, ch:ch + 1]),
                       reads=[br] + RV, writes=[hr[ch]])
                for ch in range(KC):
                    steps.append(lambda ch=ch: sA(ch))

                def sB(m):
                    pb = []
                    for part in range(3):
                        bk, br = nbank()
                        c0 = part * D + m * 128
                        pe_group([lambda kc=kc, c0=c0, bk=bk: nc.tensor.matmul(bk[:, :], Win[:, kc, c0:c0 + 128], h[:, kc, :],
                                                                               start=(kc == 0), stop=(kc == KC - 1))
                                  for kc in range(KC)], reads=Winr + hr, writes=[br])
                        pb.append((bk, br))
                    (bB, rB), (bC, rC), (bV, rV) = pb
                    vs, vsr = VSb[vstate[0] % 2], VSr[vstate[0] % 2]
                    cv, cvr = CV[vstate[0] % 2], CVr[vstate[0] % 2]
                    vstate[0] += 1
                    VV = nc.vector
                    op(ACT, lambda: nc.scalar.activation(out=vs[:, :], in_=bV[:, :], func=AF.Copy), reads=[rV], writes=[vsr])
                    op(DVE, lambda: VV.tensor_tensor(out=U[:, m, 2:TT + 2], in0=bC[:, :], in1=vs[:, :], op=ALU.mult),
                       reads=[rC, vsr], writes=[Ur[m]])
                    op(DVE, lambda: VV.tensor_scalar(out=cv[:, :], in0=U[:, m, 2:TT + 2], scalar1=cvw[:, 16 + m:17 + m],
                                                     scalar2=None, op0=ALU.mult), reads=[Ur[m]] + RV, writes=[cvr])
                    op(DVE, lambda: VV.scalar_tensor_tensor(out=cv[:, :], in0=U[:, m, 1:TT + 1], scalar=cvw[:, 8 + m:9 + m],
                                                            in1=cv[:, :], op0=ALU.mult, op1=ALU.add),
                       reads=[Ur[m]] + RV, writes=[cvr])
                    op(DVE, lambda: VV.scalar_tensor_tensor(out=cv[:, :], in0=U[:, m, 0:TT], scalar=cvw[:, m:m + 1],
                                                            in1=cv[:, :], op0=ALU.mult, op1=ALU.add),
                       reads=[Ur[m]] + RV, writes=[cvr])
                    op(DVE, lambda: VV.tensor_copy(out=U[:, m, 0:2], in_=U[:, m, TT:TT + 2]), writes=[Ur[m]])
                    op(DVE, lambda: VV.tensor_tensor(out=Z[:, m, :], in0=bB[:, :], in1=cv[:, :], op=ALU.mult),
                       reads=[rB, cvr], writes=[Zr[m]])
                for m in range(KC):
                    steps.append(lambda m=m: sB(m))

                def sC(m):
                    bk, br = nbank()
                    pe_group([lambda kc=kc: nc.tensor.matmul(bk[:, :], Wout[:, kc, m * 128:(m + 1) * 128], Z[:, kc, :],
                                                             start=(kc == 0), stop=(kc == KC - 1))
                              for kc in range(KC)], reads=Woutr + Zr, writes=[br])
                    resid(ax, axr, RB, RBr, m, bk, br, g1)
                for m in range(KC):
                    steps.append(lambda m=m: sC(m))
                return steps

            def back_steps(t):
                ax, axr = AXs[t % 2], AXr[t % 2]
                steps = ln_steps(ax, axr, RB, RBr, RS, RSr, pvec(1), pvec(2), pvec(3), pvec(4), H2, H2r)

                def s_out():
                    dma("aout%d" % (t % 2), "a1_%d" % t, AXS[0][t], flat(ax), reads=axr, writes=[AXSres[0][t]])
                    dma("hout%d" % (t % 2), "h2_%d" % t, HS[0][t], flat(H2), reads=H2r, writes=[HSres[0][t]])
                steps.append(s_out)
                return steps

            bg = []
            for t in range(NTR):
                interleave(front_steps(t), bg, start=1)
                while bg:
                    bg.pop(0)()
                bg = back_steps(t)
            while bg:
                bg.pop(0)()

    def stage_ffn(l, ax_in, axr_in, h_in, hr_in, ax_out, axr_out, gvec, aG, aB, final):
        with ExitStack() as es:
            Wg = es.enter_context(SBT("Wg", [128, KC, DFF], BF16))
            Wu = es.enter_context(SBT("Wu", [128, KC, DFF], BF16))
            Wd = es.enter_context(SBT("Wd", [128, NFC, D], BF16))
            CGS = [(0, 1024), (1024, 2048), (2048, DFF)]
            Wgr = [[Res() for _ in CGS] for _ in range(KC)]
            Wur = [[Res() for _ in CGS] for _ in range(KC)]
            Wdr = [Res() for _ in range(NFC)]
            AX = es.enter_context(SBT("AX", [128, KC, TT], F32))
            AXr = [Res() for _ in range(KC)]
            stg = [AX[:, 2 * i:2 * i + 2, :].rearrange("p a b -> p (a b)") for i in range(4)]
            sres = [Res() for _ in range(4)]
            pieces = []
            for cg, (n0, n1) in enumerate(CGS):
                for W_, Wr_, src_ in ((Wg, Wgr, w_gate[l]), (Wu, Wur, w_up[l])):
                    for kc in range(KC):
                        pieces.append((W_, Wr_[kc][cg], src_[kc * 128:(kc + 1) * 128, n0:n1], kc, n0, n1))
            for f in range(NFC):
                pieces.append((Wd, Wdr[f], w_down[l][f * 128:(f + 1) * 128, :], f, 0, D))
            pstate = {"i": 0}

            def emit_pieces(upto):
                while pstate["i"] < min(upto, len(pieces)):
                    i = pstate["i"]
                    pstate["i"] += 1
                    dst, dres, src, kc, n0, n1 = pieces[i]
                    si = i % 4
                    dma("wstg%d" % si, "fw%d_%d" % (l, i), stg[si][:, 0:n1 - n0], src, writes=[sres[si]],
                        q=(POOL if i % 2 == 0 else SP))
                    if i % 2 == 0:
                        op(DVE, lambda: nc.vector.tensor_copy(out=dst[:, kc, n0:n1], in_=stg[si][:, 0:n1 - n0]),
                           reads=[sres[si]], writes=[dres])
                    else:
                        op(ACT, lambda: nc.scalar.activation(out=dst[:, kc, n0:n1], in_=stg[si][:, 0:n1 - n0], func=AF.Copy),
                           reads=[sres[si]], writes=[dres])
            emit_pieces(16)
            Hs = [es.enter_context(SBT("H%d" % i, [128, KC, TT], BF16)) for i in range(2)]
            Hr = [[Res() for _ in range(KC)] for _ in range(2)]
            Gt = es.enter_context(SBT("G", [128, NFC, TT], BF16))
            Gr = [Res() for _ in range(NFC)]
            SL = [es.enter_context(SBT("SL%d" % i, [128, TT], F32)) for i in range(2)]
            SLr = [Res() for _ in range(2)]
            RS = es.enter_context(SBT("RS", [128, TT], F32))
            RSr = Res()
            if final:
                OS = [es.enter_context(SBT("OS%d" % i, [128, 512], F32)) for i in range(2)]
                OSr = [Res() for _ in range(2)]

            def load_h(t):
                dma("hin%d" % (t % 2), "hin%d_%d" % (l, t), flat(Hs[t % 2]), h_in[t], reads=[hr_in[t]], writes=Hr[t % 2])
            load_h(0)
            cnts = {"si": 0, "oi": 0}

            def front_steps(t):
                h, hr = Hs[t % 2], Hr[t % 2]
                steps = []

                def sF(f):
                    cg = f // 8
                    if t == 0:
                        emit_pieces(16 * (cg + 1))
                    bg_, rg = nbank()
                    bu, ru = nbank()
                    pe_group([lambda kc=kc: nc.tensor.matmul(bg_[:, :], Wg[:, kc, f * 128:(f + 1) * 128], h[:, kc, :],
                                                             start=(kc == 0), stop=(kc == KC - 1)) for kc in range(KC)],
                             reads=[Wgr[kc][cg] for kc in range(KC)] + hr, writes=[rg])
                    pe_group([lambda kc=kc: nc.tensor.matmul(bu[:, :], Wu[:, kc, f * 128:(f + 1) * 128], h[:, kc, :],
                                                             start=(kc == 0), stop=(kc == KC - 1)) for kc in range(KC)],
                             reads=[Wur[kc][cg] for kc in range(KC)] + hr, writes=[ru])
                    if t == 0:
                        emit_pieces(pstate["i"] + 3)
                    sl, slr = SL[cnts["si"] % 2], SLr[cnts["si"] % 2]
                    cnts["si"] += 1
                    op(ACT, lambda: nc.scalar.activation(out=sl[:, :], in_=bg_[:, :], func=AF.Silu), reads=[rg], writes=[slr])
                    op(DVE, lambda: nc.vector.tensor_tensor(out=Gt[:, f, :], in0=bu[:, :], in1=sl[:, :], op=ALU.mult),
                       reads=[ru, slr], writes=[Gr[f]])
                for f in range(NFC):
                    steps.append(lambda f=f: sF(f))
                return steps

            def down_steps(t):
                h, hr = Hs[t % 2], Hr[t % 2]
                steps = []

                def sD(m):
                    if m == 0:
                        if t == 0:
                            emit_pieces(len(pieces))
                        dma("ain", "ain%d_%d" % (l, t), flat(AX), ax_in[t], reads=[axr_in[t]],
                            writes=AXr + (sres if t == 0 else []))
                    bk, br = nbank()
                    pe_group([lambda f=f: nc.tensor.matmul(bk[:, :], Wd[:, f, m * 128:(m + 1) * 128], Gt[:, f, :],
                                                           start=(f == 0), stop=(f == NFC - 1)) for f in range(NFC)],
                             reads=Wdr + Gr, writes=[br])
                    resid(AX, AXr, h, hr, m, bk, br, gvec)
                for m in range(KC):
                    steps.append(lambda m=m: sD(m))
                return steps

            def back_steps(t):
                h, hr = Hs[t % 2], Hr[t % 2]
                steps = ln_steps(AX, AXr, h, hr, RS, RSr, aG, aB)
                if not final:
                    def s_out():
                        dma("aout%d" % (t % 2), "ao%d_%d" % (l, t), ax_out[t], flat(AX), reads=AXr, writes=[axr_out[t]])
                        if t + 2 < NTR:
                            load_h(t + 2)
                    steps.append(s_out)
                else:
                    def s_tr(s_, half):
                        bk, br = nbank()
                        pe_group([lambda q=q: nc.tensor.transpose(
                            bk[:, q * 128:(q + 1) * 128], AX[:, half * 4 + q, s_ * 128:(s_ + 1) * 128], ident[:, :])
                            for q in range(4)], reads=AXr + [Rconst], writes=[br])
                        oi = cnts["oi"]
                        cnts["oi"] += 1
                        o, orr = OS[oi % 2], OSr[oi % 2]
                        if oi % 2 == 0:
                            op(ACT, lambda: nc.scalar.activation(out=o[:, :], in_=bk[:, :], func=AF.Copy),
                               reads=[br], writes=[orr])
                        else:
                            op(DVE, lambda: nc.vector.tensor_copy(out=o[:, :], in_=bk[:, :]), reads=[br], writes=[orr])
                        r0 = t * TT + s_ * 128
                        dma("oout%d" % (oi % 2), "o%d" % oi, out[r0:r0 + 128, half * 512:(half + 1) * 512], o[:, :],
                            reads=[orr])
                        if s_ == 3 and half == 1 and t + 2 < NTR:
                            load_h(t + 2)
                    for s_ in range(4):
                        for half in range(2):
                            steps.append(lambda s_=s_, half=half: s_tr(s_, half))
                return steps

            if NTR > 1:
                load_h(1)
            bg = []
            for t in range(NTR):
                interleave(front_steps(t), bg, start=1)
                while bg:
                    bg.pop(0)()
                for f_ in down_steps(t):
                    f_()
                bg = back_steps(t)
            while bg:
                bg.pop(0)()

    NFT = A("NFT", [128, (S // 128) * NH], F32)
    NFTr = Res()
    FBd = nc.dram_tensor("fbd", [NH, S], BF16).ap()
    FBdr = [Res() for _ in range(NT)]
    HQd = nc.dram_tensor("hqd", [NT, 128, KC * TT], BF16).ap()
    HQdr = [Res() for _ in range(NT)]

    def stage_kv(pre=None):
        with ExitStack() as es:
            sbt = lambda *a: es.enter_context(SBT(*a))
            Wk = sbt("Wk", [128, KC, D], BF16)
            Wv = sbt("Wv", [128, KC, D], BF16)
            Wf = sbt("Wf", [128, KC, NH], BF16)
            Wff = sbt("Wff", [128, KC, NH], F32)
            Wkr, Wvr = ([Res() for _ in range(KC)] for _ in range(2))
            Wfr = Res()
            with ExitStack() as es2:
                stg = [es2.enter_context(SBT("wstg%d" % i, [128, 1024], F32)) for i in range(6)]
                sres = [Res() for _ in range(6)]
                load_w((stg, sres), Wk, Wkr, w_k, D, D, wstate)
                load_w((stg, sres), Wv, Wvr, w_v, D, D, wstate)
                barrier()
            Rwff = Res()
            dma("small", "wf", Wff[:, :, :], w_f.rearrange("(kc p) n -> p kc n", p=128), writes=[Rwff])
            op(DVE, lambda: nc.vector.tensor_copy(out=Wf[:, :, :], in_=Wff[:, :, :]), reads=[Rwff], writes=[Wfr])
            AXs = [sbt("AXk%d" % i, [128, KC, TT], F32) for i in range(2)]
            AXr = [[Res() for _ in range(KC)] for _ in range(2)]
            HKVs = [sbt("HKV%d" % i, [128, KC, TT], BF16) for i in range(2)]
            HKVr = [[Res() for _ in range(KC)] for _ in range(2)]
            KTs = [sbt("KT%d" % i, [128, KC, TT], BF16) for i in range(2)]
            KTr = [Res() for _ in range(2)]
            VST = [sbt("VST%d" % i, [128, NH, 128], BF16) for i in range(3)]
            VSTr = [Res() for _ in range(3)]
            LF = sbt("LF", [NH, TT], F32)
            LFr = Res()
            NF = sbt("NF", [NH, TT], F32)
            NFr = Res()
            ONE16 = sbt("ONE16", [NH, TT], F32)
            CAR = sbt("CAR", [NH, 1], F32)
            CARr = Res()
            FB = sbt("FB", [NH, TT], BF16)
            FBr = Res()
            V = nc.vector
            op(DVE, lambda: V.memset(ONE16[:, :], 1.0), writes=[CARr])
            op(DVE, lambda: V.memset(CAR[:, :], 0.0), writes=[CARr])
            for i in range(3):
                op(POOL, lambda i=i: nc.gpsimd.memset(VST[i][:, :, 64:128], 1.0), writes=[VSTr[i]])
            kvA, kvB = pvec(14), kvvec(0)
            KS4 = KSd.rearrange("(pr a) d t -> a d pr t", a=2)
            VS4 = VSd.rearrange("h p j c -> p h j c")
            vcnt = 0
            ecnt = 0

            def load_ax(t):
                dma("ain%d" % (t % 2), "aink_%d" % t, flat(AXs[t % 2]), AXS[1][t], reads=[AXSres[1][t]], writes=AXr[t % 2])
            HQo = [sbt("HQo%d" % i, [128, KC, TT], BF16) for i in range(2)]
            HQor = [[Res() for _ in range(KC)] for _ in range(2)]
            qA_, qB_ = pvec(13), mvec(1, 0)
            pstg = [sbt("pstg%d" % i, [128, 1024], F32) for i in range(6)]
            psres = [Res() for _ in range(6)]
            ppieces = []
            if pre is not None:
                for W_, Wr_, src_ in ((pre[0], pre[2], w_q), (pre[1], pre[3], w_o)):
                    for kc in range(KC):
                        ppieces.append((W_, Wr_[kc], src_[kc * 128:(kc + 1) * 128, :], kc))
            pp = {"d": 0, "c": 0}

            def pre_point(last=False):
                while True:
                    progressed = False
                    if pp["d"] < len(ppieces) and pp["d"] < pp["c"] + 3:
                        i = pp["d"]
                        pp["d"] += 1
                        si = i % 6
                        dma("pstg%d" % si, "pw%d" % i, pstg[si][:, :], ppieces[i][2], writes=[psres[si]],
                            q=(POOL if i % 2 == 0 else SP))
                        progressed = True
                    if pp["c"] < pp["d"] and (last or pp["c"] + 2 < pp["d"]):
                        i = pp["c"]
                        pp["c"] += 1
                        si = i % 6
                        dst, dres, _, kc = ppieces[i]
                        if i % 2 == 0:
                            op(DVE, lambda: V.tensor_copy(out=dst[:, kc, :], in_=pstg[si][:, :]), reads=[psres[si]], writes=[dres])
                        else:
                            op(ACT, lambda: nc.scalar.activation(out=dst[:, kc, :], in_=pstg[si][:, :], func=AF.Copy),
                               reads=[psres[si]], writes=[dres])
                        progressed = True
                    if not last or not progressed:
                        break

            def make_hkv(c):
                AX, axr = AXs[c % 2], AXr[c % 2]
                HKV, hkvr = HKVs[c % 2], HKVr[c % 2]
                hqo, hqor = HQo[c % 2], HQor[c % 2]
                for ch in range(KC):
                    if ch % 2 == 1:
                        op(ACT, lambda ch=ch: nc.scalar.activation(out=hqo[:, ch, :], in_=AX[:, ch, :], func=AF.Identity,
                                                                   scale=qA_[:, ch:ch + 1], bias=qB_[:, ch:ch + 1]),
                           reads=[axr[ch]] + RV, writes=[hqor[ch]])
                    else:
                        op(DVE, lambda ch=ch: V.tensor_scalar(out=hqo[:, ch, :], in0=AX[:, ch, :], scalar1=qA_[:, ch:ch + 1],
                                                              scalar2=qB_[:, ch:ch + 1], op0=ALU.mult, op1=ALU.add),
                           reads=[axr[ch]] + RV, writes=[hqor[ch]])
                dma("hqout%d" % (c % 2), "hqo%d" % c, HQd[c], flat(hqo), reads=hqor, writes=[HQdr[c]])
                for ch in range(KC):
                    if ch % 2 == 0:
                        op(ACT, lambda ch=ch: nc.scalar.activation(out=HKV[:, ch, :], in_=AX[:, ch, :], func=AF.Identity,
                                                                   scale=kvA[:, ch:ch + 1], bias=kvB[:, ch:ch + 1]),
                           reads=[axr[ch]] + RV, writes=[hkvr[ch]])
                    else:
                        op(DVE, lambda ch=ch: V.tensor_scalar(out=HKV[:, ch, :], in0=AX[:, ch, :], scalar1=kvA[:, ch:ch + 1],
                                                              scalar2=kvB[:, ch:ch + 1], op0=ALU.mult, op1=ALU.add),
                           reads=[axr[ch]] + RV, writes=[hkvr[ch]])

            load_ax(0)
            if NTR > 1:
                load_ax(1)
            make_hkv(0)
            for c in range(NTR):
                HKV, hkvr = HKVs[c % 2], HKVr[c % 2]
                KT, ktr = KTs[c % 2], KTr[c % 2]
                if c + 1 < NTR:
                    make_hkv(c + 1)
                if c + 2 < NTR:
                    load_ax(c + 2)
                bk, br = nbank()
                pe_group([lambda kc=kc: nc.tensor.matmul(bk[0:NH, :], Wf[:, kc, :], HKV[:, kc, :],
                                                         start=(kc == 0), stop=(kc == KC - 1)) for kc in range(KC)],
                         reads=[Wfr] + hkvr, writes=[br])
                op(ACT, lambda: nc.scalar.activation(out=LF[:, :], in_=bk[0:NH, :], func=AF.Exp, scale=-1.0, bias=nbf[:, 0:1]),
                   reads=[br, Rc2], writes=[LFr])
                op(ACT, lambda: nc.scalar.activation(out=LF[:, :], in_=LF[:, :], func=AF.Ln, bias=1.0), reads=[LFr], writes=[LFr])
                op(DVE, lambda: V.tensor_tensor_scan(out=NF[:, :], data0=ONE16[:, :], data1=LF[:, :], initial=CAR[:, 0:1],
                                                     op0=ALU.mult, op1=ALU.add), reads=[LFr, CARr], writes=[NFr])
                op(DVE, lambda: V.tensor_copy(out=CAR[:, 0:1], in_=NF[:, TT - 1:TT]), reads=[NFr], writes=[CARr])
                op(DVE, lambda: V.tensor_scalar(out=FB[:, :], in0=NF[:, :], scalar1=-1.0, scalar2=None, op0=ALU.mult),
                   reads=[NFr], writes=[FBr])
                dma("fbout", "fb%d" % c, FBd[:, c * TT:(c + 1) * TT], FB[:, :], reads=[FBr], writes=[FBdr[c]])
                for pr in range(KC):
                    bk, br = nbank()
                    pe_group([lambda kc=kc, bk=bk: nc.tensor.matmul(bk[:, :], Wk[:, kc, pr * 128:(pr + 1) * 128], HKV[:, kc, :],
                                                                    start=(kc == 0), stop=(kc == KC - 1)) for kc in range(KC)],
                             reads=Wkr + hkvr, writes=[br])
                    if ecnt % 2 == 0:
                        op(ACT, lambda: nc.scalar.activation(out=KT[:, pr, :], in_=bk[:, :], func=AF.Copy), reads=[br], writes=[ktr])
                    else:
                        op(DVE, lambda: V.tensor_copy(out=KT[:, pr, :], in_=bk[:, :]), reads=[br], writes=[ktr])
                    ecnt += 1
                dma("kout%d" % (c % 2), "ko%d" % c, [(KS4[a, :, :, c * TT:(c + 1) * TT], KT[a * 64:(a + 1) * 64, :, :]) for a in range(2)],
                    None, reads=[ktr], writes=[KSres[c]])
                bk2, br2 = nbank()
                pe_group([lambda s_=s_: nc.tensor.transpose(bk2[:, s_ * NH:(s_ + 1) * NH], NF[0:NH, s_ * 128:(s_ + 1) * 128],
                                                            ident[0:NH, 0:NH]) for s_ in range(4)],
                         reads=[NFr, Rconst], writes=[br2])
                op(DVE, lambda: V.tensor_copy(out=NFT[:, c * 4 * NH:(c + 1) * 4 * NH], in_=bk2[:, 0:4 * NH]),
                   reads=[br2], writes=[NFTr])
                for s_ in range(4):
                    vs_, vsr_ = VST[vcnt % 3], VSTr[vcnt % 3]
                    for half in range(2):
                        bk, br = nbank()
                        pe_group([lambda kc=kc, bk=bk: nc.tensor.matmul(bk[:, :], HKV[:, kc, s_ * 128:(s_ + 1) * 128],
                                                                        Wv[:, kc, half * 512:(half + 1) * 512],
                                                                        start=(kc == 0), stop=(kc == KC - 1)) for kc in range(KC)],
                                 reads=Wvr + hkvr, writes=[br])
                        src = bk[:, :].rearrange("p (h d) -> p h d", d=64)
                        if ecnt % 2 == 0:
                            op(ACT, lambda: nc.scalar.activation(out=vs_[:, half * 8:half * 8 + 8, 0:64], in_=src, func=AF.Copy),
                               reads=[br], writes=[vsr_])
                        else:
                            op(DVE, lambda: V.tensor_copy(out=vs_[:, half * 8:half * 8 + 8, 0:64], in_=src), reads=[br], writes=[vsr_])
                        ecnt += 1
                    dma("vout%d" % (vcnt % 3), "vo%d" % vcnt, VS4[:, :, c * 4 + s_, :], vs_[:, :, :],
                        reads=[vsr_], writes=[VSres[c]])
                    vcnt += 1
                    if s_ % 2 == 1:
                        pre_point()
                pre_point()
                if c == NTR - 1:
                    pre_point(last=True)

    def stage_attn(pre):
        with ExitStack() as es:
            sbt = lambda *a: es.enter_context(SBT(*a))
            Wq, Wo, Wqr, Wor = pre
            AX1 = sbt("AXa", [128, KC, TT], F32)
            AX1r = [Res() for _ in range(KC)]
            RBa = sbt("RBa", [128, KC, TT], BF16)
            RBar = [Res() for _ in range(KC)]
            HQs = [sbt("HQ%d" % i, [128, KC, TT], BF16) for i in range(2)]
            HQr = [[Res() for _ in range(KC)] for _ in range(2)]
            QTs = [sbt("QT%d" % i, [65, NH, TT], BF16) for i in range(2)]
            QTr = [[Res() for _ in range(NH)] for _ in range(2)]
            OTs = [sbt("OT%d" % i, [128, KC, TT], BF16) for i in range(2)]
            OTrs = [[Res() for _ in range(KC)] for _ in range(2)]
            H4 = sbt("H4", [128, KC, TT], BF16)
            H4r = [Res() for _ in range(KC)]
            KB = [sbt("KB%d" % i, [65, S], BF16) for i in range(3)]
            KBr = [Res() for _ in range(3)]
            VB = [sbt("VB%d" % i, [128, S // 128, 128], BF16) for i in range(3)]
            VBr = [Res() for _ in range(3)]
            PT = [sbt("PT%d" % i, [128, TT], BF16) for i in range(5)]
            PTr = [Res() for _ in range(5)]
            RL = sbt("RL", [128, TT], F32)
            RLr = Res()
            RS = sbt("RS", [128, TT], F32)
            RSr = Res()
            OBS = [sbt("OBS%d" % i, [128, TT], F32) for i in range(2)]
            OBSr = [Res() for _ in range(2)]
            V = nc.vector
            for i in range(3):
                op(POOL, lambda i=i: nc.gpsimd.memset(KB[i][64:65, :], 1.0), writes=[KBr[i]])
            qA, qB = pvec(13), mvec(1, 0)
            g1 = mvec(1, 2)
            hcnt = 0
            pcnt = 0

            def load_kv(c, h, slot):
                n = (c + 1) * TT
                dma("kin%d" % slot, "k%d_%d" % (c, h), KB[slot][0:64, 0:n], KSd[h, :, 0:n],
                    reads=KSres[0:c + 1], writes=[KBr[slot]])
                dma("vin%d" % slot, "v%d_%d" % (c, h), VB[slot][:, 0:4 * (c + 1), :], VSd[h, :, 0:4 * (c + 1), :],
                    reads=VSres[0:c + 1], writes=[VBr[slot]])

            def front_steps(c):
                HQ, hqr = HQs[c % 2], HQr[c % 2]
                QT, qtr = QTs[c % 2], QTr[c % 2]
                steps = []

                def s0():
                    dma("hqin%d" % (c % 2), "hqin_%d" % c, flat(HQ), HQd[c], reads=[HQdr[c]], writes=hqr)
                    dma("qrow", "qr%d" % c, [(QT[64:65, h, :], FBd[h:h + 1, c * TT:(c + 1) * TT]) for h in range(NH)], None,
                        reads=[FBdr[c]], writes=qtr)
                steps.append(s0)
                steps += [lambda: None] * 3

                def s_q(pr):
                    bk, br = nbank()
                    pe_group([lambda kc=kc: nc.tensor.matmul(bk[:, :], Wq[:, kc, pr * 128:(pr + 1) * 128], HQ[:, kc, :],
                                                             start=(kc == 0), stop=(kc == KC - 1)) for kc in range(KC)],
                             reads=Wqr + hqr, writes=[br])
                    op(DVE, lambda: V.tensor_scalar(out=QT[0:64, 2 * pr, :], in0=bk[0:64, :], scalar1=0.125, scalar2=None,
                                                    op0=ALU.mult), reads=[br], writes=[qtr[2 * pr]])
                    op(DVE, lambda: V.tensor_scalar(out=QT[0:64, 2 * pr + 1, :], in0=bk[64:128, :], scalar1=0.125, scalar2=None,
                                                    op0=ALU.mult), reads=[br], writes=[qtr[2 * pr + 1]])
                for pr in range(KC):
                    steps.append(lambda pr=pr: s_q(pr))
                return steps

            def back_steps(c):
                AX, axr = AX1, AX1r
                HQ, hqr = RBa, RBar
                OT, OTr = OTs[c % 2], OTrs[c % 2]
                steps = []

                def s_ld():
                    dma("ain", "aina_%d" % c, flat(AX), AXS[1][c], reads=[AXSres[1][c]], writes=axr)
                steps.append(s_ld)

                def s_wo(m):
                    bk, br = nbank()
                    pe_group([lambda kc=kc: nc.tensor.matmul(bk[:, :], Wo[:, kc, m * 128:(m + 1) * 128], OT[:, kc, :],
                                                             start=(kc == 0), stop=(kc == KC - 1)) for kc in range(KC)],
                             reads=Wor + OTr, writes=[br])
                    resid(AX, axr, HQ, hqr, m, bk, br, g1, rb_dve=True)
                for m in range(KC):
                    steps.append(lambda m=m: s_wo(m))
                ln = ln_steps(AX, axr, HQ, hqr, RS, RSr, pvec(6), pvec(7), pvec(8), pvec(9), H4, H4r, dve_out=True)
                sp = lambda: None
                steps += [sp] + ln[0:1] + ln[1:9] + [sp] + ln[9:10] + [sp] + ln[10:]

                def s_out():
                    dma("aout%d" % (c % 2), "a3_%d" % c, AXS[2][c], flat(AX), reads=axr, writes=[AXSres[2][c]])
                    dma("hout%d" % (c % 2), "h4_%d" % c, HS[1][c], flat(H4), reads=H4r, writes=[HSres[1][c]])
                steps.append(s_out)
                return steps

            for f in front_steps(0):
                f()
            GH = [(c_, h_) for c_ in range(NTR) for h_ in range(NH)]

            def issue(g):
                if g < len(GH):
                    load_kv(GH[g][0], GH[g][1], g % 3)
            for g in range(3):
                issue(g)
            bg = []
            for c in range(NTR):
                QT, qtr = QTs[c % 2], QTr[c % 2]
                OT, OTr = OTs[c % 2], OTrs[c % 2]
                nj = 4 * c + 4
                LOOK = 4
                hslot = {}
                hob = {}
                pend = []
                hbase = c * NH
                if c + 1 < NTR:
                    bg = front_steps(c + 1) + bg
                usable = (NH - 1) * nj - 2
                every = max(1, min(4, usable // (len(bg) + 1)))

                def qk(h, j):
                    nonlocal pcnt
                    slot = hslot[h]
                    kb = KB[slot]
                    r = max(0, j - 4 * c)
                    c0 = 128 * r
                    sb_, sr_ = nbank()
                    fns = [lambda: nc.tensor.matmul(sb_[:, c0:TT], kb[0:65, j * 128:(j + 1) * 128], QT[0:65, h, c0:TT],
                                                    start=True, stop=(j < 4 * c))]
                    if j >= 4 * c:
                        fns.append(lambda: nc.tensor.matmul(sb_[:, c0:TT], identb[:, :], maskb[:, 0:TT - c0],
                                                            start=False, stop=True))
                    pe_group(fns, reads=[KBr[slot], qtr[h], Rc2], writes=[sr_])
                    pt, ptr = PT[pcnt % 5], PTr[pcnt % 5]
                    pcnt += 1
                    col = j * NH + h
                    op(ACT, lambda: nc.scalar.activation(out=pt[:, c0:TT], in_=sb_[:, c0:TT], func=AF.Exp,
                                                         bias=NFT[:, col:col + 1], scale=1.0),
                       reads=[sr_, NFTr], writes=[ptr])
                    return (h, j, c0, pt, ptr)

                def pvm(item):
                    h, j, c0, pt, ptr = item
                    slot = hslot[h]
                    ob, obr = hob[h]
                    pe_group([lambda: nc.tensor.matmul(ob[:, c0:TT], VB[slot][:, j, :], pt[:, c0:TT],
                                                       start=(j == 0), stop=(j == nj - 1))],
                             reads=[VBr[slot], ptr], writes=[obr])
                    if j == nj - 1:
                        obs, obsr = OBS[h % 2], OBSr[h % 2]
                        op(DVE, lambda: V.tensor_copy(out=obs[:, :], in_=ob[:, :]), reads=[obr], writes=[obsr])
                        release(ob)
                        if c <= 2:
                            op(ACT, lambda: nc.scalar.activation(out=RL[0:64, :], in_=obs[64:128, :], func=AF.Ln),
                               reads=[obsr], writes=[RLr])
                            op(ACT, lambda: nc.scalar.activation(out=RL[0:64, :], in_=RL[0:64, :], func=AF.Exp, scale=-1.0),
                               reads=[RLr], writes=[RLr])
                        else:
                            op(DVE, lambda: V.reciprocal(out=RL[0:64, :], in_=obs[64:128, :]), reads=[obsr], writes=[RLr])
                        d0 = (h % 2) * 64
                        op(DVE, lambda: V.tensor_tensor(out=OT[d0:d0 + 64, h // 2, :], in0=obs[0:64, :], in1=RL[0:64, :],
                                                        op=ALU.mult), reads=[obsr, RLr], writes=[OTr[h // 2]])
                        issue(hbase + h + 3)

                it = 0
                for h in range(NH):
                    hslot[h] = (hbase + h) % 3
                    hob[h] = nbank(hold=True)
                    for j in range(nj):
                        pend.append(qk(h, j))
                        if len(pend) > LOOK:
                            pvm(pend.pop(0))
                        it += 1
                        if bg and it >= 2 and h < NH - 1 and (it - 2) % every == 0:
                            bg.pop(0)()
                while pend:
                    pvm(pend.pop(0))
                while bg:
                    bg.pop(0)()
                bg += back_steps(c)
            while bg:
                bg.pop(0)()

    order = ["mod", "conv", "ffn0", "attn", "all"]
    lvl = order.index(upto)
    stage_mod()
    barrier()
    if lvl >= 1:
        stage_conv()
        barrier()
    if lvl >= 2:
        stage_ffn(0, AXS[0], AXSres[0], HS[0], HSres[0], AXS[1], AXSres[1], mvec(0, 5), pvec(11), pvec(12), final=False)
        barrier()
    if lvl >= 3:
        with ExitStack() as es0:
            pre = (es0.enter_context(SBT("Wq", [128, KC, D], BF16)), es0.enter_context(SBT("Wo", [128, KC, D], BF16)),
                   [Res() for _ in range(KC)], [Res() for _ in range(KC)])
            stage_kv(pre)
            barrier()
            stage_attn(pre)
            barrier()
    if lvl >= 4:
        stage_ffn(1, AXS[2], AXSres[2], HS[1], HSres[1], None, None, mvec(1, 5), lvec(2, 1), lvec(3, 1), final=True)
    if upto != "all":
        dbg = nc.dram_tensor("dbg", [NT, 128, KC * TT], F32, kind="ExternalOutput").ap()
        dbg2 = nc.dram_tensor("dbg2", [128, 112 + NV * 8], F32, kind="ExternalOutput").ap()
        dma("small", "dbg2", dbg2[:, 0:112], modT[:, :], reads=[Rmod, Rpv])
        dma("small", "dbg2", dbg2[:, 112:112 + NV * 8], pv[:, :], reads=[Rmod, Rpv])
        if lvl >= 1:
            src = AXS[lvl - 1]
            for t in range(NTR):
                if "noout" in SKIP:
                    break
                dma("small", "dbg", dbg[t], src[t], reads=[AXSres[lvl - 1][t]])
    for name, sm in dma_sems.items():
        SP.wait(sm, sm.n)
    return nc


_NC = None


def _get_nc():
    global _NC
    if _NC is None:
        _NC = build()
    return _NC


def kernel(x, c, w_ada, b_ada, conv_w_in, conv_w, conv_w_out, w_ada_kv, b_ada_kv,
           w_k, w_v, w_f, b_f, attn_w_q, attn_w_o, ffn_w_gate, ffn_w_up, ffn_w_down,
           ln1_g, ln1_b, ln2_g, ln2_b):
    f = lambda a: np.ascontiguousarray(np.asarray(a), dtype=np.float32)
    x = f(x); c = f(c)
    B = x.shape[0]
    b_all = np.concatenate([f(b_ada).reshape(-1), f(b_ada_kv).reshape(-1)])[None, :]
    convw = f(conv_w)[0].reshape(3, 8, 128).transpose(2, 0, 1).reshape(128, 24)
    lnv = np.stack([f(ln1_g), f(ln1_b), f(ln2_g), f(ln2_b)]).reshape(4, 2, 8, 128).transpose(3, 0, 1, 2).reshape(128, 64)
    ident = np.eye(128, dtype=np.float32)
    kk = np.arange(128)[:, None]
    qq = np.arange(512)[None, :]
    trimask = np.where(qq >= kk, 0.0, NEG).astype(np.float32)
    shared = {
        "w_ada": f(w_ada), "b_all": f(b_all), "w_ada_kv": f(w_ada_kv),
        "conv_w_in": f(conv_w_in)[0], "convw": f(convw), "conv_w_out": f(conv_w_out)[0],
        "w_k": f(w_k), "w_v": f(w_v), "w_f": f(w_f), "b_f": f(b_f).reshape(NH, 1),
        "w_q": f(attn_w_q)[0], "w_o": f(attn_w_o)[0],
        "w_gate": f(ffn_w_gate), "w_up": f(ffn_w_up), "w_down": f(ffn_w_down),
        "lnv": f(lnv), "ident": ident, "trimask": trimask,
    }
    in_maps = []
    for b in range(B):
        m = dict(shared)
        m["x"] = x[b]
        m["cT"] = f(c[b].reshape(8, 128).T)
        in_maps.append(m)
    nc = _get_nc()
    res = run_bass_kernel_spmd(nc, in_maps, core_ids=list(range(B)))
    return np.stack([np.asarray(r["out"], dtype=np.float32).reshape(S, D) for r in res.results], axis=0)
```

```python
import numpy as np
from contextlib import ExitStack
import concourse.bass as bass
import concourse.mybir as mybir
from concourse.bass_utils import run_bass_kernel_spmd

F32 = mybir.dt.float32
BF16 = mybir.dt.bfloat16
AF = mybir.ActivationFunctionType
ALU = mybir.AluOpType

D = 1024
S = 4096
NH = 16
DFF = 2816
NFC = DFF // 128
TT = 512
NT = S // TT
NTR = NT
SKIP = set()
KC = D // 128
ALPHA = float((2.0 * 2) ** 0.25)
EPS = 1e-5
NEG = -30000.0


class Sm:
    def __init__(self, nc, name):
        self.h = nc.alloc_semaphore(name)
        self.n = 0
        self.tag = None


class Eng:
    def __init__(self, nc, e, name):
        self.e = e
        self.sem = Sm(nc, "es_" + name)
        self.waited = {}
        self.name = name

    def wait(self, sm, val):
        if val <= 0 or self.waited.get(sm, 0) >= val:
            return
        self.e.wait_ge(sm.h, val)
        self.waited[sm] = val


class Res:
    __slots__ = ("w", "rs")

    def __init__(self):
        self.w = {}
        self.rs = {}


def build(upto="all"):
    nc = bass.Bass("TRN2", target_bir_lowering=False)

    def din(name, shape, dt=F32):
        return nc.dram_tensor(name, shape, dt, kind="ExternalInput").ap()

    x = din("x", [S, D])
    cT = din("cT", [128, 8])
    w_ada = din("w_ada", [2, D, 6 * D])
    b_all = din("b_all", [1, 14336])
    w_ada_kv = din("w_ada_kv", [D, 2 * D])
    conv_w_in = din("conv_w_in", [D, 3 * D])
    convw = din("convw", [128, 24])
    conv_w_out = din("conv_w_out", [D, D])
    w_k = din("w_k", [D, D])
    w_v = din("w_v", [D, D])
    w_f = din("w_f", [D, NH])
    b_f = din("b_f", [NH, 1])
    w_q = din("w_q", [D, D])
    w_o = din("w_o", [D, D])
    w_gate = din("w_gate", [2, D, DFF])
    w_up = din("w_up", [2, D, DFF])
    w_down = din("w_down", [2, DFF, D])
    lnv = din("lnv", [128, 64])
    ident_in = din("ident", [128, 128])
    mask_in = din("trimask", [128, 512])
    out = nc.dram_tensor("out", [S, D], F32, kind="ExternalOutput").ap()

    modscr = nc.dram_tensor("modscr", [14336], F32).ap()
    AXS = [nc.dram_tensor("axs%d" % i, [NT, 128, KC * TT], F32).ap() for i in range(3)]
    HS = [nc.dram_tensor("hs%d" % i, [NT, 128, KC * TT], BF16).ap() for i in range(2)]
    KSd = nc.dram_tensor("ksd", [NH, 64, S], BF16).ap()
    VSd = nc.dram_tensor("vsd", [NH, 128, S // 128, 128], BF16).ap()
    AXSres = [[Res() for _ in range(NT)] for _ in range(3)]
    HSres = [[Res() for _ in range(NT)] for _ in range(2)]
    KSres = [Res() for _ in range(NT)]
    VSres = [Res() for _ in range(NT)]

    PE = Eng(nc, nc.tensor, "pe")
    ACT = Eng(nc, nc.scalar, "act")
    DVE = Eng(nc, nc.vector, "dve")
    POOL = Eng(nc, nc.gpsimd, "pool")
    SP = Eng(nc, nc.sync, "sp")

    def op(eng, fn, reads=(), writes=()):
        own = eng.sem
        for r in reads:
            for sm, v in r.w.items():
                if not (eng is PE and sm is own):
                    eng.wait(sm, v)
        for r in writes:
            for sm, v in r.w.items():
                if not (eng is PE and sm is own):
                    eng.wait(sm, v)
            for sm, v in r.rs.items():
                if not (eng is PE and sm is own):
                    eng.wait(sm, v)
        ins = fn()
        own.n += 1
        ins.then_inc(own.h, 1)
        for r in reads:
            r.rs[own] = own.n
        for r in writes:
            r.w[own] = own.n
            r.rs = {}
        return ins

    def pe_group(fns, reads, writes):
        def run():
            ins = None
            for f in fns:
                ins = f()
            return ins
        return op(PE, run, reads, writes)

    dma_sems = {}

    def dsem(name):
        if name not in dma_sems:
            dma_sems[name] = Sm(nc, "dq_" + name)
        return dma_sems[name]

    def dma(semname, tag, out_ap, in_ap, reads=(), writes=(), q=None, **kw):
        q = q or SP
        sm = dsem(semname)
        for r in reads:
            for s_, v in r.w.items():
                q.wait(s_, v)
        for r in writes:
            for s_, v in r.w.items():
                q.wait(s_, v)
            for s_, v in r.rs.items():
                q.wait(s_, v)
        if sm.tag is not None and sm.tag != tag:
            q.wait(sm, sm.n)
        sm.tag = tag
        pairs = out_ap if isinstance(out_ap, list) else [(out_ap, in_ap)]
        for o_, i_ in pairs:
            ins = q.e.dma_start(out=o_, in_=i_, **kw)
            ins.then_inc(sm.h, 16)
            sm.n += 16
        for r in reads:
            r.rs[sm] = sm.n
        for r in writes:
            r.w[sm] = sm.n
            r.rs = {}

    uid = [0]

    def SBT(name, shape, dt):
        uid[0] += 1
        return nc.sbuf_tensor("%s_%d" % (name, uid[0]), shape, dt)

    def barrier():
        engs = [PE, ACT, DVE, POOL, SP]
        sems = [e.sem for e in engs if e is not SP] + list(dma_sems.values())
        for e in engs:
            for sm in sems:
                if sm is not e.sem or e is not PE:
                    e.wait(sm, sm.n)

    banks = [nc.alloc_psum_tensor("bank%d" % i, [128, 512], F32) for i in range(8)]
    bank_res = [Res() for _ in range(8)]
    ring = {"i": 0}

    held = set()

    def nbank(hold=False):
        i = ring["i"]
        while i in held:
            i = (i + 1) % 8
        ring["i"] = (i + 1) % 8
        if hold:
            held.add(i)
        return banks[i], bank_res[i]

    def release(bank):
        held.discard(banks.index(bank))

    A = nc.alloc_sbuf_tensor
    ident = A("ident_sb", [128, 128], F32)
    identb = A("identb", [128, 128], BF16)
    onesb = A("onesb", [128, 128], BF16)
    maskf = A("maskf", [128, 512], F32)
    maskb = A("maskb", [128, 512], BF16)
    modT = A("modT", [128, 112], F32)
    lnvt = A("lnvt", [128, 64], F32)
    cvw = A("cvw", [128, 24], F32)
    cin = A("cin", [128, 8], F32)
    cact = A("cact", [128, 8], F32)
    nbf = A("nbf", [NH, 1], F32)
    NV = 16
    pv = A("pv", [128, NV * 8], F32)
    Rconst = Res()
    Rmod = Res()
    Rpv = Res()
    op(DVE, lambda: nc.vector.memset(pv[:], 0.0), writes=[Rpv])

    dma("small", "c0", ident[:], ident_in, writes=[Rconst])
    dma("small", "c0", maskf[:], mask_in, writes=[Rconst])
    dma("small", "c0", lnvt[:], lnv, writes=[Rconst])
    dma("small", "c0", cvw[:], convw, writes=[Rconst])
    dma("small", "c0", cin[:], cT, writes=[Rconst])
    dma("small", "c0", nbf[:], b_f, writes=[Rconst])
    Rc2 = Res()
    op(DVE, lambda: nc.vector.tensor_copy(out=identb[:], in_=ident[:]), reads=[Rconst], writes=[Rc2])
    op(DVE, lambda: nc.vector.tensor_copy(out=maskb[:], in_=maskf[:]), reads=[Rconst], writes=[Rc2])
    op(DVE, lambda: nc.vector.memset(onesb[:], 1.0), writes=[Rc2])
    epsc = A("epsc", [128, 1], F32)
    op(DVE, lambda: nc.vector.memset(epsc[:], EPS), writes=[Rc2])
    op(DVE, lambda: nc.vector.tensor_scalar(out=nbf[:], in0=nbf[:], scalar1=-1.0, scalar2=None, op0=ALU.mult),
       reads=[Rconst], writes=[Rc2])
    op(ACT, lambda: nc.scalar.activation(out=cact[:], in_=cin[:], func=AF.Silu), reads=[Rconst], writes=[Rc2])

    def mvec(l, v):
        return modT[:, l * 48 + v * 8: l * 48 + v * 8 + 8]

    def kvvec(v):
        return modT[:, 96 + v * 8: 96 + v * 8 + 8]

    def lvec(v, l):
        return lnvt[:, (v * 2 + l) * 8: (v * 2 + l) * 8 + 8]

    def pvec(i):
        return pv[:, i * 8: i * 8 + 8]

    def stage_mod():
        with ExitStack() as es:
            stg = [es.enter_context(SBT("mstg%d" % i, [128, 1024], F32)) for i in range(6)]
            stg_res = [Res() for _ in range(6)]
            brow = es.enter_context(SBT("brow", [1, 14336], F32))
            mrow = es.enter_context(SBT("mrow", [1, 14336], F32))
            Rb = Res()
            Rm = Res()
            dma("small", "brow", brow[:], b_all, writes=[Rb])
            cnt = 0
            for g in range(7):
                bk = [nbank() for _ in range(4)]
                for kc in range(KC):
                    for half in range(2):
                        si = cnt % 6
                        cnt += 1
                        if g < 6:
                            c_lo = (g % 3) * 2048 + half * 1024
                            src = w_ada[g // 3, kc * 128:(kc + 1) * 128, c_lo:c_lo + 1024]
                        else:
                            src = w_ada_kv[kc * 128:(kc + 1) * 128, half * 1024:(half + 1) * 1024]
                        dma("wstg%d" % si, "m%d" % cnt, stg[si][:], src, writes=[stg_res[si]],
                            q=(SP if cnt % 2 == 0 else POOL))
                        fns = []
                        for b2 in range(2):
                            b = half * 2 + b2
                            fns.append(lambda b=b, b2=b2, kc=kc, si=si: nc.tensor.matmul(
                                bk[b][0][0:1, :], cact[:, kc:kc + 1], stg[si][:, b2 * 512:(b2 + 1) * 512],
                                start=(kc == 0), stop=(kc == KC - 1)))
                        pe_group(fns, reads=[stg_res[si], Rc2], writes=[bk[half * 2][1], bk[half * 2 + 1][1]])
                for b in range(4):
                    c0 = g * 2048 + b * 512
                    op(DVE, lambda b=b, c0=c0: nc.vector.tensor_tensor(
                        out=mrow[0:1, c0:c0 + 512], in0=bk[b][0][0:1, :], in1=brow[0:1, c0:c0 + 512], op=ALU.add),
                       reads=[bk[b][1], Rb], writes=[Rm])
            Rscr = Res()
            dma("small", "mscr", modscr.rearrange("(o n) -> o n", o=1), mrow[:], reads=[Rm], writes=[Rscr])
            dma("small", "mscr2", modT[:], modscr.rearrange("(c p) -> p c", p=128), reads=[Rscr], writes=[Rmod],
                allow_slow_non_contiguous=True)
        V = nc.vector
        def dv(fn):
            op(DVE, fn, reads=[Rmod, Rconst, Rpv], writes=[Rpv])
        dv(lambda: V.tensor_scalar(out=pvec(0), in0=mvec(0, 1), scalar1=1.0, scalar2=None, op0=ALU.add))
        for l in range(2):
            b = 1 + l * 5
            dv(lambda l=l, b=b: V.tensor_scalar(out=pvec(b), in0=lvec(0, l), scalar1=ALPHA, scalar2=None, op0=ALU.mult))
            dv(lambda l=l, b=b: V.tensor_scalar(out=pvec(b + 1), in0=lvec(1, l), scalar1=ALPHA, scalar2=None, op0=ALU.mult))
            dv(lambda l=l, b=b: V.tensor_scalar(out=pvec(b + 4), in0=mvec(l, 4), scalar1=1.0, scalar2=None, op0=ALU.add))
            dv(lambda l=l, b=b: V.tensor_tensor(out=pvec(b + 2), in0=lvec(0, l), in1=pvec(b + 4), op=ALU.mult))
            dv(lambda l=l, b=b: V.tensor_tensor(out=pvec(b + 3), in0=lvec(1, l), in1=pvec(b + 4), op=ALU.mult))
            dv(lambda l=l, b=b: V.tensor_tensor(out=pvec(b + 3), in0=pvec(b + 3), in1=mvec(l, 3), op=ALU.add))
        dv(lambda: V.tensor_scalar(out=pvec(11), in0=lvec(2, 0), scalar1=ALPHA, scalar2=None, op0=ALU.mult))
        dv(lambda: V.tensor_scalar(out=pvec(12), in0=lvec(3, 0), scalar1=ALPHA, scalar2=None, op0=ALU.mult))
        dv(lambda: V.tensor_scalar(out=pvec(13), in0=mvec(1, 1), scalar1=1.0, scalar2=1.0 / ALPHA, op0=ALU.add, op1=ALU.mult))
        dv(lambda: V.tensor_scalar(out=pvec(14), in0=kvvec(1), scalar1=1.0, scalar2=1.0 / ALPHA, op0=ALU.add, op1=ALU.mult))

    RV = [Rpv, Rmod, Rconst, Rc2]

    def load_w(es_stg, dst, dst_res, src, K, N, state):
        stg, stg_res = es_stg
        ns = len(stg)
        for kc in range(K // 128):
            eng = [DVE, ACT][state[0] % 2]
            state[0] += 1
            for n0 in range(0, N, 1024):
                n1 = min(N, n0 + 1024)
                si = state[1] % ns
                state[1] += 1
                dma("wstg%d" % si, "w%d" % state[1], stg[si][:, 0:n1 - n0], src[kc * 128:(kc + 1) * 128, n0:n1],
                    writes=[stg_res[si]], q=(SP if state[1] % 2 == 0 else POOL))
                if eng is DVE:
                    op(DVE, lambda si=si, kc=kc, n0=n0, n1=n1: nc.vector.tensor_copy(out=dst[:, kc, n0:n1], in_=stg[si][:, 0:n1 - n0]),
                       reads=[stg_res[si]], writes=[dst_res[kc]])
                else:
                    op(ACT, lambda si=si, kc=kc, n0=n0, n1=n1: nc.scalar.activation(out=dst[:, kc, n0:n1], in_=stg[si][:, 0:n1 - n0],
                                                                                   func=AF.Copy),
                       reads=[stg_res[si]], writes=[dst_res[kc]])

    wstate = [0, 0]

    def ln_steps(AX, AXres, RB, RBres, RS, RSres, aG, aB, hA=None, hB=None, H=None, Hres=None, dve_out=False):
        V = nc.vector
        st = {}
        steps = []

        def s_stats1():
            st["mb"], st["mr"] = nbank(hold=True)
            mb = st["mb"]
            pe_group([lambda m=m: nc.tensor.matmul(mb[:, :], onesb[:, :], RB[:, m, :], start=(m == 0), stop=(m == KC - 1))
                      for m in range(KC)], reads=[RBres[m] for m in range(KC)] + [Rc2], writes=[st["mr"]])
        steps.append(s_stats1)

        def s_center(m):
            mb, mr = st["mb"], st["mr"]
            op(DVE, lambda: V.scalar_tensor_tensor(out=AX[:, m, :], in0=mb[:, :], scalar=-1.0 / D, in1=AX[:, m, :],
                                                   op0=ALU.mult, op1=ALU.add), reads=[mr], writes=[AXres[m]])
            op(DVE, lambda: V.tensor_tensor(out=RB[:, m, :], in0=AX[:, m, :], in1=AX[:, m, :], op=ALU.mult),
               reads=[AXres[m]], writes=[RBres[m]])
            if m == KC - 1:
                release(mb)
        for m in range(KC):
            steps.append(lambda m=m: s_center(m))

        def s_stats2():
            vb, vr = nbank()
            pe_group([lambda m=m: nc.tensor.matmul(vb[:, :], onesb[:, :], RB[:, m, :], start=(m == 0), stop=(m == KC - 1))
                      for m in range(KC)], reads=[RBres[m] for m in range(KC)] + [Rc2], writes=[vr])
            op(ACT, lambda: nc.scalar.activation(out=RS[:, :], in_=vb[:, :], func=AF.Ln, scale=1.0 / D, bias=epsc[:, 0:1]),
               reads=[vr, Rc2], writes=[RSres])
            op(ACT, lambda: nc.scalar.activation(out=RS[:, :], in_=RS[:, :], func=AF.Exp, scale=-0.5),
               reads=[RSres], writes=[RSres])
        steps.append(s_stats2)

        def s_out(m):
            op(DVE, lambda: V.tensor_tensor(out=AX[:, m, :], in0=AX[:, m, :], in1=RS[:, :], op=ALU.mult),
               reads=[RSres], writes=[AXres[m]])
            if H is not None:
                if dve_out:
                    op(DVE, lambda: V.tensor_scalar(out=H[:, m, :], in0=AX[:, m, :], scalar1=hA[:, m:m + 1],
                                                    scalar2=hB[:, m:m + 1], op0=ALU.mult, op1=ALU.add),
                       reads=[AXres[m]] + RV, writes=[Hres[m]])
                else:
                    op(ACT, lambda: nc.scalar.activation(out=H[:, m, :], in_=AX[:, m, :], func=AF.Identity,
                                                         scale=hA[:, m:m + 1], bias=hB[:, m:m + 1]),
                       reads=[AXres[m]] + RV, writes=[Hres[m]])
            if dve_out:
                op(DVE, lambda: V.tensor_scalar(out=AX[:, m, :], in0=AX[:, m, :], scalar1=aG[:, m:m + 1],
                                                scalar2=aB[:, m:m + 1], op0=ALU.mult, op1=ALU.add),
                   reads=RV, writes=[AXres[m]])
            else:
                op(ACT, lambda: nc.scalar.activation(out=AX[:, m, :], in_=AX[:, m, :], func=AF.Identity,
                                                     scale=aG[:, m:m + 1], bias=aB[:, m:m + 1]),
                   reads=RV, writes=[AXres[m]])
        for m in range(KC):
            steps.append(lambda m=m: s_out(m))
        return steps

    def finish_ln(*a, **kw):
        for f in ln_steps(*a, **kw):
            f()

    def resid(AX, AXres, RB, RBres, m, bank, bres, gvec, rb_dve=False):
        op(DVE, lambda: nc.vector.scalar_tensor_tensor(out=AX[:, m, :], in0=bank[:, :], scalar=gvec[:, m:m + 1],
                                                       in1=AX[:, m, :], op0=ALU.mult, op1=ALU.add),
           reads=[bres] + RV, writes=[AXres[m]])
        if rb_dve:
            op(DVE, lambda: nc.vector.tensor_copy(out=RB[:, m, :], in_=AX[:, m, :]), reads=[AXres[m]], writes=[RBres[m]])
        else:
            op(ACT, lambda: nc.scalar.activation(out=RB[:, m, :], in_=AX[:, m, :], func=AF.Copy),
               reads=[AXres[m]], writes=[RBres[m]])

    def flat(t):
        return t[:, :, :].rearrange("p a b -> p (a b)")

    def interleave(main, bg, start=1, every=1):
        for i, f in enumerate(main):
            f()
            if i >= start and bg and (i - start) % every == 0:
                bg.pop(0)()

    def stage_conv():
        with ExitStack() as es:
            Win = es.enter_context(SBT("Win", [128, KC, 3 * D], BF16))
            Wout = es.enter_context(SBT("Wout", [128, KC, D], BF16))
            Winr = [Res() for _ in range(KC)]
            Woutr = [Res() for _ in range(KC)]
            with ExitStack() as es2:
                stg = [es2.enter_context(SBT("wstg%d" % i, [128, 1024], F32)) for i in range(6)]
                sres = [Res() for _ in range(6)]
                load_w((stg, sres), Win, Winr, conv_w_in, D, 3 * D, wstate)
                load_w((stg, sres), Wout, Woutr, conv_w_out, D, D, wstate)
                barrier()
            XT = [es.enter_context(SBT("XT%d" % i, [128, 4, D], F32)) for i in range(2)]
            XTr = [Res() for _ in range(2)]
            AXs = [es.enter_context(SBT("AX%d" % i, [128, KC, TT], F32)) for i in range(2)]
            AXr = [[Res() for _ in range(KC)] for _ in range(2)]
            Hs = [es.enter_context(SBT("H%d" % i, [128, KC, TT], BF16)) for i in range(2)]
            Hr = [[Res() for _ in range(KC)] for _ in range(2)]
            H2 = es.enter_context(SBT("H2", [128, KC, TT], BF16))
            H2r = [Res() for _ in range(KC)]
            Z = es.enter_context(SBT("Z", [128, KC, TT], BF16))
            Zr = [Res() for _ in range(KC)]
            RB = es.enter_context(SBT("RB", [128, KC, TT], BF16))
            RBr = [Res() for _ in range(KC)]
            U = es.enter_context(SBT("U", [128, KC, TT + 2], F32))
            Ur = [Res() for _ in range(KC)]
            VSb = [es.enter_context(SBT("VSb%d" % i, [128, TT], F32)) for i in range(2)]
            VSr = [Res() for _ in range(2)]
            CV = [es.enter_context(SBT("CV%d" % i, [128, TT], F32)) for i in range(2)]
            CVr = [Res() for _ in range(2)]
            RS = es.enter_context(SBT("RS", [128, TT], F32))
            RSr = Res()
            T1 = es.enter_context(SBT("T1", [128, TT], F32))
            T1r = Res()
            for m in range(KC):
                if "umem" in SKIP:
                    break
                op(POOL, lambda m=m: nc.gpsimd.memset(U[:, m, 0:2], 0.0), writes=[Ur[m]])

            def load_x(t):
                dma("xin%d" % (t % 2), "x%d" % t, XT[t % 2][:, :, :],
                    x[t * TT:(t + 1) * TT, :].rearrange("(s p) d -> p s d", p=128), writes=[XTr[t % 2]])
            if "loadx" not in SKIP:
                load_x(0)
            A0 = pvec(0)
            B0 = mvec(0, 0)
            g1 = mvec(0, 2)
            vstate = [0]

            def front_steps(t):
                xt = XT[t % 2]
                ax, axr = AXs[t % 2], AXr[t % 2]
                h, hr = Hs[t % 2], Hr[t % 2]
                steps = []

                def sA(ch):
                    if ch == 0 and t + 1 < NTR:
                        load_x(t + 1)
                    bk, br = nbank()
                    pe_group([lambda s_=s_: nc.tensor.transpose(bk[:, s_ * 128:(s_ + 1) * 128],
                                                                xt[:, s_, ch * 128:(ch + 1) * 128], ident[:, :])
                              for s_ in range(4)], reads=[XTr[t % 2], Rconst], writes=[br])
                    op(ACT, lambda: nc.scalar.activation(out=ax[:, ch, :], in_=bk[:, :], func=AF.Copy, scale=ALPHA),
                       reads=[br], writes=[axr[ch]])
                    op(ACT, lambda: nc.scalar.activation(out=h[:, ch, :], in_=bk[:, :], func=AF.Identity,
                                                         scale=A0[:, ch:ch + 1], bias=B0[:, ch:ch + 1]),
                       reads=[br] + RV, writes=[hr[ch]])
                for ch in range(KC):
                    steps.append(lambda ch=ch: sA(ch))

                def sB(m):
                    pb = []
                    for part in range(3):
                        bk, br = nbank()
                        c0 = part * D + m * 128
                        pe_group([lambda kc=kc, c0=c0, bk=bk: nc.tensor.matmul(bk[:, :], Win[:, kc, c0:c0 + 128], h[:, kc, :],
                                                                               start=(kc == 0), stop=(kc == KC - 1))
                                  for kc in range(KC)], reads=Winr + hr, writes=[br])
                        pb.append((bk, br))
                    (bB, rB), (bC, rC), (bV, rV) = pb
                    vs, vsr = VSb[vstate[0] % 2], VSr[vstate[0] % 2]
                    cv, cvr = CV[vstate[0] % 2], CVr[vstate[0] % 2]
                    vstate[0] += 1
                    VV = nc.vector
                    op(ACT, lambda: nc.scalar.activation(out=vs[:, :], in_=bV[:, :], func=AF.Copy), reads=[rV], writes=[vsr])
                    op(DVE, lambda: VV.tensor_tensor(out=U[:, m, 2:TT + 2], in0=bC[:, :], in1=vs[:, :], op=ALU.mult),
                       reads=[rC, vsr], writes=[Ur[m]])
                    op(DVE, lambda: VV.tensor_scalar(out=cv[:, :], in0=U[:, m, 2:TT + 2], scalar1=cvw[:, 16 + m:17 + m],
                                                     scalar2=None, op0=ALU.mult), reads=[Ur[m]] + RV, writes=[cvr])
                    op(DVE, lambda: VV.scalar_tensor_tensor(out=cv[:, :], in0=U[:, m, 1:TT + 1], scalar=cvw[:, 8 + m:9 + m],
                                                            in1=cv[:, :], op0=ALU.mult, op1=ALU.add),
                       reads=[Ur[m]] + RV, writes=[cvr])
                    op(DVE, lambda: VV.scalar_tensor_tensor(out=cv[:, :], in0=U[:, m, 0:TT], scalar=cvw[:, m:m + 1],
                                                            in1=cv[:, :], op0=ALU.mult, op1=ALU.add),
                       reads=[Ur[m]] + RV, writes=[cvr])
                    op(DVE, lambda: VV.tensor_copy(out=U[:, m, 0:2], in_=U[:, m, TT:TT + 2]), writes=[Ur[m]])
                    op(DVE, lambda: VV.tensor_tensor(out=Z[:, m, :], in0=bB[:, :], in1=cv[:, :], op=ALU.mult),
                       reads=[rB, cvr], writes=[Zr[m]])
                for m in range(KC):
                    steps.append(lambda m=m: sB(m))

                def sC(m):
                    bk, br = nbank()
                    pe_group([lambda kc=kc: nc.tensor.matmul(bk[:, :], Wout[:, kc, m * 128:(m + 1) * 128], Z[:, kc, :],
                                                             start=(kc == 0), stop=(kc == KC - 1))
                              for kc in range(KC)], reads=Woutr + Zr, writes=[br])
                    resid(ax, axr, RB, RBr, m, bk, br, g1)
                for m in range(KC):
                    steps.append(lambda m=m: sC(m))
                return steps

            def back_steps(t):
                ax, axr = AXs[t % 2], AXr[t % 2]
                steps = ln_steps(ax, axr, RB, RBr, RS, RSr, pvec(1), pvec(2), pvec(3), pvec(4), H2, H2r)

                def s_out():
                    dma("aout%d" % (t % 2), "a1_%d" % t, AXS[0][t], flat(ax), reads=axr, writes=[AXSres[0][t]])
                    dma("hout%d" % (t % 2), "h2_%d" % t, HS[0][t], flat(H2), reads=H2r, writes=[HSres[0][t]])
                steps.append(s_out)
                return steps

            bg = []
            for t in range(NTR):
                interleave(front_steps(t), bg, start=1)
                while bg:
                    bg.pop(0)()
                bg = back_steps(t)
            while bg:
                bg.pop(0)()

    def stage_ffn(l, ax_in, axr_in, h_in, hr_in, ax_out, axr_out, gvec, aG, aB, final):
        with ExitStack() as es:
            Wg = es.enter_context(SBT("Wg", [128, KC, DFF], BF16))
            Wu = es.enter_context(SBT("Wu", [128, KC, DFF], BF16))
            Wd = es.enter_context(SBT("Wd", [128, NFC, D], BF16))
            CGS = [(0, 1024), (1024, 2048), (2048, DFF)]
            Wgr = [[Res() for _ in CGS] for _ in range(KC)]
            Wur = [[Res() for _ in CGS] for _ in range(KC)]
            Wdr = [Res() for _ in range(NFC)]
            AX = es.enter_context(SBT("AX", [128, KC, TT], F32))
            AXr = [Res() for _ in range(KC)]
            stg = [AX[:, 2 * i:2 * i + 2, :].rearrange("p a b -> p (a b)") for i in range(4)]
            sres = [Res() for _ in range(4)]
            pieces = []
            for cg, (n0, n1) in enumerate(CGS):
                for W_, Wr_, src_ in ((Wg, Wgr, w_gate[l]), (Wu, Wur, w_up[l])):
                    for kc in range(KC):
                        pieces.append((W_, Wr_[kc][cg], src_[kc * 128:(kc + 1) * 128, n0:n1], kc, n0, n1))
            for f in range(NFC):
                pieces.append((Wd, Wdr[f], w_down[l][f * 128:(f + 1) * 128, :], f, 0, D))
            pstate = {"i": 0}

            def emit_pieces(upto):
                while pstate["i"] < min(upto, len(pieces)):
                    i = pstate["i"]
                    pstate["i"] += 1
                    dst, dres, src, kc, n0, n1 = pieces[i]
                    si = i % 4
                    dma("wstg%d" % si, "fw%d_%d" % (l, i), stg[si][:, 0:n1 - n0], src, writes=[sres[si]],
                        q=(POOL if i % 2 == 0 else SP))
                    if i % 2 == 0:
                        op(DVE, lambda: nc.vector.tensor_copy(out=dst[:, kc, n0:n1], in_=stg[si][:, 0:n1 - n0]),
                           reads=[sres[si]], writes=[dres])
                    else:
                        op(ACT, lambda: nc.scalar.activation(out=dst[:, kc, n0:n1], in_=stg[si][:, 0:n1 - n0], func=AF.Copy),
                           reads=[sres[si]], writes=[dres])
            emit_pieces(16)
            Hs = [es.enter_context(SBT("H%d" % i, [128, KC, TT], BF16)) for i in range(2)]
            Hr = [[Res() for _ in range(KC)] for _ in range(2)]
            Gt = es.enter_context(SBT("G", [128, NFC, TT], BF16))
            Gr = [Res() for _ in range(NFC)]
            SL = [es.enter_context(SBT("SL%d" % i, [128, TT], F32)) for i in range(2)]
            SLr = [Res() for _ in range(2)]
            RS = es.enter_context(SBT("RS", [128, TT], F32))
            RSr = Res()
            if final:
                OS = [es.enter_context(SBT("OS%d" % i, [128, 512], F32)) for i in range(2)]
                OSr = [Res() for _ in range(2)]

            def load_h(t):
                dma("hin%d" % (t % 2), "hin%d_%d" % (l, t), flat(Hs[t % 2]), h_in[t], reads=[hr_in[t]], writes=Hr[t % 2])
            load_h(0)
            cnts = {"si": 0, "oi": 0}

            def front_steps(t):
                h, hr = Hs[t % 2], Hr[t % 2]
                steps = []

                def sF(f):
                    cg = f // 8
                    if t == 0:
                        emit_pieces(16 * (cg + 1))
                    bg_, rg = nbank()
                    bu, ru = nbank()
                    pe_group([lambda kc=kc: nc.tensor.matmul(bg_[:, :], Wg[:, kc, f * 128:(f + 1) * 128], h[:, kc, :],
                                                             start=(kc == 0), stop=(kc == KC - 1)) for kc in range(KC)],
                             reads=[Wgr[kc][cg] for kc in range(KC)] + hr, writes=[rg])
                    pe_group([lambda kc=kc: nc.tensor.matmul(bu[:, :], Wu[:, kc, f * 128:(f + 1) * 128], h[:, kc, :],
                                                             start=(kc == 0), stop=(kc == KC - 1)) for kc in range(KC)],
                             reads=[Wur[kc][cg] for kc in range(KC)] + hr, writes=[ru])
                    if t == 0:
                        emit_pieces(pstate["i"] + 3)
                    sl, slr = SL[cnts["si"] % 2], SLr[cnts["si"] % 2]
                    cnts["si"] += 1
                    op(ACT, lambda: nc.scalar.activation(out=sl[:, :], in_=bg_[:, :], func=AF.Silu), reads=[rg], writes=[slr])
                    op(DVE, lambda: nc.vector.tensor_tensor(out=Gt[:, f, :], in0=bu[:, :], in1=sl[:, :], op=ALU.mult),
                       reads=[ru, slr], writes=[Gr[f]])
                for f in range(NFC):
                    steps.append(lambda f=f: sF(f))
                return steps

            def down_steps(t):
                h, hr = Hs[t % 2], Hr[t % 2]
                steps = []

                def sD(m):
                    if m == 0:
                        if t == 0:
                            emit_pieces(len(pieces))
                        dma("ain", "ain%d_%d" % (l, t), flat(AX), ax_in[t], reads=[axr_in[t]],
                            writes=AXr + (sres if t == 0 else []))
                    bk, br = nbank()
                    pe_group([lambda f=f: nc.tensor.matmul(bk[:, :], Wd[:, f, m * 128:(m + 1) * 128], Gt[:, f, :],
                                                           start=(f == 0), stop=(f == NFC - 1)) for f in range(NFC)],
                             reads=Wdr + Gr, writes=[br])
                    resid(AX, AXr, h, hr, m, bk, br, gvec)
                for m in range(KC):
                    steps.append(lambda m=m: sD(m))
                return steps

            def back_steps(t):
                h, hr = Hs[t % 2], Hr[t % 2]
                steps = ln_steps(AX, AXr, h, hr, RS, RSr, aG, aB)
                if not final:
                    def s_out():
                        dma("aout%d" % (t % 2), "ao%d_%d" % (l, t), ax_out[t], flat(AX), reads=AXr, writes=[axr_out[t]])
                        if t + 2 < NTR:
                            load_h(t + 2)
                    steps.append(s_out)
                else:
                    def s_tr(s_, half):
                        bk, br = nbank()
                        pe_group([lambda q=q: nc.tensor.transpose(
                            bk[:, q * 128:(q + 1) * 128], AX[:, half * 4 + q, s_ * 128:(s_ + 1) * 128], ident[:, :])
                            for q in range(4)], reads=AXr + [Rconst], writes=[br])
                        oi = cnts["oi"]
                        cnts["oi"] += 1
                        o, orr = OS[oi % 2], OSr[oi % 2]
                        if oi % 2 == 0:
                            op(ACT, lambda: nc.scalar.activation(out=o[:, :], in_=bk[:, :], func=AF.Copy),
                               reads=[br], writes=[orr])
                        else:
                            op(DVE, lambda: nc.vector.tensor_copy(out=o[:, :], in_=bk[:, :]), reads=[br], writes=[orr])
                        r0 = t * TT + s_ * 128
                        dma("oout%d" % (oi % 2), "o%d" % oi, out[r0:r0 + 128, half * 512:(half + 1) * 512], o[:, :],
                            reads=[orr])
                        if s_ == 3 and half == 1 and t + 2 < NTR:
                            load_h(t + 2)
                    for s_ in range(4):
                        for half in range(2):
                            steps.append(lambda s_=s_, half=half: s_tr(s_, half))
                return steps

            if NTR > 1:
                load_h(1)
            bg = []
            for t in range(NTR):
                interleave(front_steps(t), bg, start=1)
                while bg:
                    bg.pop(0)()
                for f_ in down_steps(t):
                    f_()
                bg = back_steps(t)
            while bg:
                bg.pop(0)()

    NFT = A("NFT", [128, (S // 128) * NH], F32)
    NFTr = Res()
    FBd = nc.dram_tensor("fbd", [NH, S], BF16).ap()
    FBdr = [Res() for _ in range(NT)]
    HQd = nc.dram_tensor("hqd", [NT, 128, KC * TT], BF16).ap()
    HQdr = [Res() for _ in range(NT)]

    def stage_kv(pre=None):
        with ExitStack() as es:
            sbt = lambda *a: es.enter_context(SBT(*a))
            Wk = sbt("Wk", [128, KC, D], BF16)
            Wv = sbt("Wv", [128, KC, D], BF16)
            Wf = sbt("Wf", [128, KC, NH], BF16)
            Wff = sbt("Wff", [128, KC, NH], F32)
            Wkr, Wvr = ([Res() for _ in range(KC)] for _ in range(2))
            Wfr = Res()
            with ExitStack() as es2:
                stg = [es2.enter_context(SBT("wstg%d" % i, [128, 1024], F32)) for i in range(6)]
                sres = [Res() for _ in range(6)]
                load_w((stg, sres), Wk, Wkr, w_k, D, D, wstate)
                load_w((stg, sres), Wv, Wvr, w_v, D, D, wstate)
                barrier()
            Rwff = Res()
            dma("small", "wf", Wff[:, :, :], w_f.rearrange("(kc p) n -> p kc n", p=128), writes=[Rwff])
            op(DVE, lambda: nc.vector.tensor_copy(out=Wf[:, :, :], in_=Wff[:, :, :]), reads=[Rwff], writes=[Wfr])
            AXs = [sbt("AXk%d" % i, [128, KC, TT], F32) for i in range(2)]
            AXr = [[Res() for _ in range(KC)] for _ in range(2)]
            HKVs = [sbt("HKV%d" % i, [128, KC, TT], BF16) for i in range(2)]
            HKVr = [[Res() for _ in range(KC)] for _ in range(2)]
            KTs = [sbt("KT%d" % i, [128, KC, TT], BF16) for i in range(2)]
            KTr = [Res() for _ in range(2)]
            VST = [sbt("VST%d" % i, [128, NH, 128], BF16) for i in range(3)]
            VSTr = [Res() for _ in range(3)]
            LF = sbt("LF", [NH, TT], F32)
            LFr = Res()
            NF = sbt("NF", [NH, TT], F32)
            NFr = Res()
            ONE16 = sbt("ONE16", [NH, TT], F32)
            CAR = sbt("CAR", [NH, 1], F32)
            CARr = Res()
            FB = sbt("FB", [NH, TT], BF16)
            FBr = Res()
            V = nc.vector
            op(DVE, lambda: V.memset(ONE16[:, :], 1.0), writes=[CARr])
            op(DVE, lambda: V.memset(CAR[:, :], 0.0), writes=[CARr])
            for i in range(3):
                op(POOL, lambda i=i: nc.gpsimd.memset(VST[i][:, :, 64:128], 1.0), writes=[VSTr[i]])
            kvA, kvB = pvec(14), kvvec(0)
            KS4 = KSd.rearrange("(pr a) d t -> a d pr t", a=2)
            VS4 = VSd.rearrange("h p j c -> p h j c")
            vcnt = 0
            ecnt = 0

            def load_ax(t):
                dma("ain%d" % (t % 2), "aink_%d" % t, flat(AXs[t % 2]), AXS[1][t], reads=[AXSres[1][t]], writes=AXr[t % 2])
            HQo = [sbt("HQo%d" % i, [128, KC, TT], BF16) for i in range(2)]
            HQor = [[Res() for _ in range(KC)] for _ in range(2)]
            qA_, qB_ = pvec(13), mvec(1, 0)
            pstg = [sbt("pstg%d" % i, [128, 1024], F32) for i in range(6)]
            psres = [Res() for _ in range(6)]
            ppieces = []
            if pre is not None:
                for W_, Wr_, src_ in ((pre[0], pre[2], w_q), (pre[1], pre[3], w_o)):
                    for kc in range(KC):
                        ppieces.append((W_, Wr_[kc], src_[kc * 128:(kc + 1) * 128, :], kc))
            pp = {"d": 0, "c": 0}

            def pre_point(last=False):
                while True:
                    progressed = False
                    if pp["d"] < len(ppieces) and pp["d"] < pp["c"] + 3:
                        i = pp["d"]
                        pp["d"] += 1
                        si = i % 6
                        dma("pstg%d" % si, "pw%d" % i, pstg[si][:, :], ppieces[i][2], writes=[psres[si]],
                            q=(POOL if i % 2 == 0 else SP))
                        progressed = True
                    if pp["c"] < pp["d"] and (last or pp["c"] + 2 < pp["d"]):
                        i = pp["c"]
                        pp["c"] += 1
                        si = i % 6
                        dst, dres, _, kc = ppieces[i]
                        if i % 2 == 0:
                            op(DVE, lambda: V.tensor_copy(out=dst[:, kc, :], in_=pstg[si][:, :]), reads=[psres[si]], writes=[dres])
                        else:
                            op(ACT, lambda: nc.scalar.activation(out=dst[:, kc, :], in_=pstg[si][:, :], func=AF.Copy),
                               reads=[psres[si]], writes=[dres])
                        progressed = True
                    if not last or not progressed:
                        break

            def make_hkv(c):
                AX, axr = AXs[c % 2], AXr[c % 2]
                HKV, hkvr = HKVs[c % 2], HKVr[c % 2]
                hqo, hqor = HQo[c % 2], HQor[c % 2]
                for ch in range(KC):
                    if ch % 2 == 1:
                        op(ACT, lambda ch=ch: nc.scalar.activation(out=hqo[:, ch, :], in_=AX[:, ch, :], func=AF.Identity,
                                                                   scale=qA_[:, ch:ch + 1], bias=qB_[:, ch:ch + 1]),
                           reads=[axr[ch]] + RV, writes=[hqor[ch]])
                    else:
                        op(DVE, lambda ch=ch: V.tensor_scalar(out=hqo[:, ch, :], in0=AX[:, ch, :], scalar1=qA_[:, ch:ch + 1],
                                                              scalar2=qB_[:, ch:ch + 1], op0=ALU.mult, op1=ALU.add),
                           reads=[axr[ch]] + RV, writes=[hqor[ch]])
                dma("hqout%d" % (c % 2), "hqo%d" % c, HQd[c], flat(hqo), reads=hqor, writes=[HQdr[c]])
                for ch in range(KC):
                    if ch % 2 == 0:
                        op(ACT, lambda ch=ch: nc.scalar.activation(out=HKV[:, ch, :], in_=AX[:, ch, :], func=AF.Identity,
                                                                   scale=kvA[:, ch:ch + 1], bias=kvB[:, ch:ch + 1]),
                           reads=[axr[ch]] + RV, writes=[hkvr[ch]])
                    else:
                        op(DVE, lambda ch=ch: V.tensor_scalar(out=HKV[:, ch, :], in0=AX[:, ch, :], scalar1=kvA[:, ch:ch + 1],
                                                              scalar2=kvB[:, ch:ch + 1], op0=ALU.mult, op1=ALU.add),
                           reads=[axr[ch]] + RV, writes=[hkvr[ch]])

            load_ax(0)
            if NTR > 1:
                load_ax(1)
            make_hkv(0)
            for c in range(NTR):
                HKV, hkvr = HKVs[c % 2], HKVr[c % 2]
                KT, ktr = KTs[c % 2], KTr[c % 2]
                if c + 1 < NTR:
                    make_hkv(c + 1)
                if c + 2 < NTR:
                    load_ax(c + 2)
                bk, br = nbank()
                pe_group([lambda kc=kc: nc.tensor.matmul(bk[0:NH, :], Wf[:, kc, :], HKV[:, kc, :],
                                                         start=(kc == 0), stop=(kc == KC - 1)) for kc in range(KC)],
                         reads=[Wfr] + hkvr, writes=[br])
                op(ACT, lambda: nc.scalar.activation(out=LF[:, :], in_=bk[0:NH, :], func=AF.Exp, scale=-1.0, bias=nbf[:, 0:1]),
                   reads=[br, Rc2], writes=[LFr])
                op(ACT, lambda: nc.scalar.activation(out=LF[:, :], in_=LF[:, :], func=AF.Ln, bias=1.0), reads=[LFr], writes=[LFr])
                op(DVE, lambda: V.tensor_tensor_scan(out=NF[:, :], data0=ONE16[:, :], data1=LF[:, :], initial=CAR[:, 0:1],
                                                     op0=ALU.mult, op1=ALU.add), reads=[LFr, CARr], writes=[NFr])
                op(DVE, lambda: V.tensor_copy(out=CAR[:, 0:1], in_=NF[:, TT - 1:TT]), reads=[NFr], writes=[CARr])
                op(DVE, lambda: V.tensor_scalar(out=FB[:, :], in0=NF[:, :], scalar1=-1.0, scalar2=None, op0=ALU.mult),
                   reads=[NFr], writes=[FBr])
                dma("fbout", "fb%d" % c, FBd[:, c * TT:(c + 1) * TT], FB[:, :], reads=[FBr], writes=[FBdr[c]])
                for pr in range(KC):
                    bk, br = nbank()
                    pe_group([lambda kc=kc, bk=bk: nc.tensor.matmul(bk[:, :], Wk[:, kc, pr * 128:(pr + 1) * 128], HKV[:, kc, :],
                                                                    start=(kc == 0), stop=(kc == KC - 1)) for kc in range(KC)],
                             reads=Wkr + hkvr, writes=[br])
                    if ecnt % 2 == 0:
                        op(ACT, lambda: nc.scalar.activation(out=KT[:, pr, :], in_=bk[:, :], func=AF.Copy), reads=[br], writes=[ktr])
                    else:
                        op(DVE, lambda: V.tensor_copy(out=KT[:, pr, :], in_=bk[:, :]), reads=[br], writes=[ktr])
                    ecnt += 1
                dma("kout%d" % (c % 2), "ko%d" % c, [(KS4[a, :, :, c * TT:(c + 1) * TT], KT[a * 64:(a + 1) * 64, :, :]) for a in range(2)],
                    None, reads=[ktr], writes=[KSres[c]])
                bk2, br2 = nbank()
                pe_group([lambda s_=s_: nc.tensor.transpose(bk2[:, s_ * NH:(s_ + 1) * NH], NF[0:NH, s_ * 128:(s_ + 1) * 128],
                                                            ident[0:NH, 0:NH]) for s_ in range(4)],
                         reads=[NFr, Rconst], writes=[br2])
                op(DVE, lambda: V.tensor_copy(out=NFT[:, c * 4 * NH:(c + 1) * 4 * NH], in_=bk2[:, 0:4 * NH]),
                   reads=[br2], writes=[NFTr])
                for s_ in range(4):
                    vs_, vsr_ = VST[vcnt % 3], VSTr[vcnt % 3]
                    for half in range(2):
                        bk, br = nbank()
                        pe_group([lambda kc=kc, bk=bk: nc.tensor.matmul(bk[:, :], HKV[:, kc, s_ * 128:(s_ + 1) * 128],
                                                                        Wv[:, kc, half * 512:(half + 1) * 512],
                                                                        start=(kc == 0), stop=(kc == KC - 1)) for kc in range(KC)],
                                 reads=Wvr + hkvr, writes=[br])
                        src = bk[:, :].rearrange("p (h d) -> p h d", d=64)
                        if ecnt % 2 == 0:
                            op(ACT, lambda: nc.scalar.activation(out=vs_[:, half * 8:half * 8 + 8, 0:64], in_=src, func=AF.Copy),
                               reads=[br], writes=[vsr_])
                        else:
                            op(DVE, lambda: V.tensor_copy(out=vs_[:, half * 8:half * 8 + 8, 0:64], in_=src), reads=[br], writes=[vsr_])
                        ecnt += 1
                    dma("vout%d" % (vcnt % 3), "vo%d" % vcnt, VS4[:, :, c * 4 + s_, :], vs_[:, :, :],
                        reads=[vsr_], writes=[VSres[c]])
                    vcnt += 1
                    if s_ % 2 == 1:
                        pre_point()
                pre_point()
                if c == NTR - 1:
                    pre_point(last=True)

    def stage_attn(pre):
        with ExitStack() as es:
            sbt = lambda *a: es.enter_context(SBT(*a))
            Wq, Wo, Wqr, Wor = pre
            AX1 = sbt("AXa", [128, KC, TT], F32)
            AX1r = [Res() for _ in range(KC)]
            RBa = sbt("RBa", [128, KC, TT], BF16)
            RBar = [Res() for _ in range(KC)]
            HQs = [sbt("HQ%d" % i, [128, KC, TT], BF16) for i in range(2)]
            HQr = [[Res() for _ in range(KC)] for _ in range(2)]
            QTs = [sbt("QT%d" % i, [65, NH, TT], BF16) for i in range(2)]
            QTr = [[Res() for _ in range(NH)] for _ in range(2)]
            OTs = [sbt("OT%d" % i, [128, KC, TT], BF16) for i in range(2)]
            OTrs = [[Res() for _ in range(KC)] for _ in range(2)]
            H4 = sbt("H4", [128, KC, TT], BF16)
            H4r = [Res() for _ in range(KC)]
            KB = [sbt("KB%d" % i, [65, S], BF16) for i in range(3)]
            KBr = [Res() for _ in range(3)]
            VB = [sbt("VB%d" % i, [128, S // 128, 128], BF16) for i in range(3)]
            VBr = [Res() for _ in range(3)]
            PT = [sbt("PT%d" % i, [128, TT], BF16) for i in range(5)]
            PTr = [Res() for _ in range(5)]
            RL = sbt("RL", [128, TT], F32)
            RLr = Res()
            RS = sbt("RS", [128, TT], F32)
            RSr = Res()
            OBS = [sbt("OBS%d" % i, [128, TT], F32) for i in range(2)]
            OBSr = [Res() for _ in range(2)]
            V = nc.vector
            for i in range(3):
                op(POOL, lambda i=i: nc.gpsimd.memset(KB[i][64:65, :], 1.0), writes=[KBr[i]])
            qA, qB = pvec(13), mvec(1, 0)
            g1 = mvec(1, 2)
            hcnt = 0
            pcnt = 0

            def load_kv(c, h, slot):
                n = (c + 1) * TT
                dma("kin%d" % slot, "k%d_%d" % (c, h), KB[slot][0:64, 0:n], KSd[h, :, 0:n],
                    reads=KSres[0:c + 1], writes=[KBr[slot]])
                dma("vin%d" % slot, "v%d_%d" % (c, h), VB[slot][:, 0:4 * (c + 1), :], VSd[h, :, 0:4 * (c + 1), :],
                    reads=VSres[0:c + 1], writes=[VBr[slot]])

            def front_steps(c):
                HQ, hqr = HQs[c % 2], HQr[c % 2]
                QT, qtr = QTs[c % 2], QTr[c % 2]
                steps = []

                def s0():
                    dma("hqin%d" % (c % 2), "hqin_%d" % c, flat(HQ), HQd[c], reads=[HQdr[c]], writes=hqr)
                    dma("qrow", "qr%d" % c, [(QT[64:65, h, :], FBd[h:h + 1, c * TT:(c + 1) * TT]) for h in range(NH)], None,
                        reads=[FBdr[c]], writes=qtr)
                steps.append(s0)
                steps += [lambda: None] * 3

                def s_q(pr):
                    bk, br = nbank()
                    pe_group([lambda kc=kc: nc.tensor.matmul(bk[:, :], Wq[:, kc, pr * 128:(pr + 1) * 128], HQ[:, kc, :],
                                                             start=(kc == 0), stop=(kc == KC - 1)) for kc in range(KC)],
                             reads=Wqr + hqr, writes=[br])
                    op(DVE, lambda: V.tensor_scalar(out=QT[0:64, 2 * pr, :], in0=bk[0:64, :], scalar1=0.125, scalar2=None,
                                                    op0=ALU.mult), reads=[br], writes=[qtr[2 * pr]])
                    op(DVE, lambda: V.tensor_scalar(out=QT[0:64, 2 * pr + 1, :], in0=bk[64:128, :], scalar1=0.125, scalar2=None,
                                                    op0=ALU.mult), reads=[br], writes=[qtr[2 * pr + 1]])
                for pr in range(KC):
                    steps.append(lambda pr=pr: s_q(pr))
                return steps

            def back_steps(c):
                AX, axr = AX1, AX1r
                HQ, hqr = RBa, RBar
                OT, OTr = OTs[c % 2], OTrs[c % 2]
                steps = []

                def s_ld():
                    dma("ain", "aina_%d" % c, flat(AX), AXS[1][c], reads=[AXSres[1][c]], writes=axr)
                steps.append(s_ld)

                def s_wo(m):
                    bk, br = nbank()
                    pe_group([lambda kc=kc: nc.tensor.matmul(bk[:, :], Wo[:, kc, m * 128:(m + 1) * 128], OT[:, kc, :],
                                                             start=(kc == 0), stop=(kc == KC - 1)) for kc in range(KC)],
                             reads=Wor + OTr, writes=[br])
                    resid(AX, axr, HQ, hqr, m, bk, br, g1, rb_dve=True)
                for m in range(KC):
                    steps.append(lambda m=m: s_wo(m))
                ln = ln_steps(AX, axr, HQ, hqr, RS, RSr, pvec(6), pvec(7), pvec(8), pvec(9), H4, H4r, dve_out=True)
                sp = lambda: None
                steps += [sp] + ln[0:1] + ln[1:9] + [sp] + ln[9:10] + [sp] + ln[10:]

                def s_out():
                    dma("aout%d" % (c % 2), "a3_%d" % c, AXS[2][c], flat(AX), reads=axr, writes=[AXSres[2][c]])
                    dma("hout%d" % (c % 2), "h4_%d" % c, HS[1][c], flat(H4), reads=H4r, writes=[HSres[1][c]])
                steps.append(s_out)
                return steps

            for f in front_steps(0):
                f()
            GH = [(c_, h_) for c_ in range(NTR) for h_ in range(NH)]

            def issue(g):
                if g < len(GH):
                    load_kv(GH[g][0], GH[g][1], g % 3)
            for g in range(3):
                issue(g)
            bg = []
            for c in range(NTR):
                QT, qtr = QTs[c % 2], QTr[c % 2]
                OT, OTr = OTs[c % 2], OTrs[c % 2]
                nj = 4 * c + 4
                LOOK = 4
                hslot = {}
                hob = {}
                pend = []
                hbase = c * NH
                if c + 1 < NTR:
                    bg = front_steps(c + 1) + bg
                usable = (NH - 1) * nj - 2
                every = max(1, min(8, usable // (len(bg) + 1)))

                def qk(h, j):
                    nonlocal pcnt
                    slot = hslot[h]
                    kb = KB[slot]
                    r = max(0, j - 4 * c)
                    c0 = 128 * r
                    sb_, sr_ = nbank()
                    fns = [lambda: nc.tensor.matmul(sb_[:, c0:TT], kb[0:65, j * 128:(j + 1) * 128], QT[0:65, h, c0:TT],
                                                    start=True, stop=(j < 4 * c))]
                    if j >= 4 * c:
                        fns.append(lambda: nc.tensor.matmul(sb_[:, c0:TT], identb[:, :], maskb[:, 0:TT - c0],
                                                            start=False, stop=True))
                    pe_group(fns, reads=[KBr[slot], qtr[h], Rc2], writes=[sr_])
                    pt, ptr = PT[pcnt % 5], PTr[pcnt % 5]
                    pcnt += 1
                    col = j * NH + h
                    op(ACT, lambda: nc.scalar.activation(out=pt[:, c0:TT], in_=sb_[:, c0:TT], func=AF.Exp,
                                                         bias=NFT[:, col:col + 1], scale=1.0),
                       reads=[sr_, NFTr], writes=[ptr])
                    return (h, j, c0, pt, ptr)

                def pvm(item):
                    h, j, c0, pt, ptr = item
                    slot = hslot[h]
                    ob, obr = hob[h]
                    pe_group([lambda: nc.tensor.matmul(ob[:, c0:TT], VB[slot][:, j, :], pt[:, c0:TT],
                                                       start=(j == 0), stop=(j == nj - 1))],
                             reads=[VBr[slot], ptr], writes=[obr])
                    if j == nj - 1:
                        obs, obsr = OBS[h % 2], OBSr[h % 2]
                        op(DVE, lambda: V.tensor_copy(out=obs[:, :], in_=ob[:, :]), reads=[obr], writes=[obsr])
                        release(ob)
                        if c <= 2:
                            op(ACT, lambda: nc.scalar.activation(out=RL[0:64, :], in_=obs[64:128, :], func=AF.Ln),
                               reads=[obsr], writes=[RLr])
                            op(ACT, lambda: nc.scalar.activation(out=RL[0:64, :], in_=RL[0:64, :], func=AF.Exp, scale=-1.0),
                               reads=[RLr], writes=[RLr])
                        else:
                            op(DVE, lambda: V.reciprocal(out=RL[0:64, :], in_=obs[64:128, :]), reads=[obsr], writes=[RLr])
                        d0 = (h % 2) * 64
                        op(DVE, lambda: V.tensor_tensor(out=OT[d0:d0 + 64, h // 2, :], in0=obs[0:64, :], in1=RL[0:64, :],
                                                        op=ALU.mult), reads=[obsr, RLr], writes=[OTr[h // 2]])
                        issue(hbase + h + 3)

                it = 0
                for h in range(NH):
                    hslot[h] = (hbase + h) % 3
                    hob[h] = nbank(hold=True)
                    for j in range(nj):
                        pend.append(qk(h, j))
                        if len(pend) > LOOK:
                            pvm(pend.pop(0))
                        it += 1
                        if bg and it >= 2 and h < NH - 1 and (it - 2) % every == 0:
                            bg.pop(0)()
                while pend:
                    pvm(pend.pop(0))
                while bg:
                    bg.pop(0)()
                bg += back_steps(c)
            while bg:
                bg.pop(0)()

    order = ["mod", "conv", "ffn0", "attn", "all"]
    lvl = order.index(upto)
    stage_mod()
    barrier()
    if lvl >= 1:
        stage_conv()
        barrier()
    if lvl >= 2:
        stage_ffn(0, AXS[0], AXSres[0], HS[0], HSres[0], AXS[1], AXSres[1], mvec(0, 5), pvec(11), pvec(12), final=False)
        barrier()
    if lvl >= 3:
        with ExitStack() as es0:
            pre = (es0.enter_context(SBT("Wq", [128, KC, D], BF16)), es0.enter_context(SBT("Wo", [128, KC, D], BF16)),
                   [Res() for _ in range(KC)], [Res() for _ in range(KC)])
            stage_kv(pre)
            barrier()
            stage_attn(pre)
            barrier()
    if lvl >= 4:
        stage_ffn(1, AXS[2], AXSres[2], HS[1], HSres[1], None, None, mvec(1, 5), lvec(2, 1), lvec(3, 1), final=True)
    if upto != "all":
        dbg = nc.dram_tensor("dbg", [NT, 128, KC * TT], F32, kind="ExternalOutput").ap()
        dbg2 = nc.dram_tensor("dbg2", [128, 112 + NV * 8], F32, kind="ExternalOutput").ap()
        dma("small", "dbg2", dbg2[:, 0:112], modT[:, :], reads=[Rmod, Rpv])
        dma("small", "dbg2", dbg2[:, 112:112 + NV * 8], pv[:, :], reads=[Rmod, Rpv])
        if lvl >= 1:
            src = AXS[lvl - 1]
            for t in range(NTR):
                if "noout" in SKIP:
                    break
                dma("small", "dbg", dbg[t], src[t], reads=[AXSres[lvl - 1][t]])
    for name, sm in dma_sems.items():
        SP.wait(sm, sm.n)
    return nc


_NC = None


def _get_nc():
    global _NC
    if _NC is None:
        _NC = build()
    return _NC


def kernel(x, c, w_ada, b_ada, conv_w_in, conv_w, conv_w_out, w_ada_kv, b_ada_kv,
           w_k, w_v, w_f, b_f, attn_w_q, attn_w_o, ffn_w_gate, ffn_w_up, ffn_w_down,
           ln1_g, ln1_b, ln2_g, ln2_b):
    f = lambda a: np.ascontiguousarray(np.asarray(a), dtype=np.float32)
    x = f(x); c = f(c)
    B = x.shape[0]
    b_all = np.concatenate([f(b_ada).reshape(-1), f(b_ada_kv).reshape(-1)])[None, :]
    convw = f(conv_w)[0].reshape(3, 8, 128).transpose(2, 0, 1).reshape(128, 24)
    lnv = np.stack([f(ln1_g), f(ln1_b), f(ln2_g), f(ln2_b)]).reshape(4, 2, 8, 128).transpose(3, 0, 1, 2).reshape(128, 64)
    ident = np.eye(128, dtype=np.float32)
    kk = np.arange(128)[:, None]
    qq = np.arange(512)[None, :]
    trimask = np.where(qq >= kk, 0.0, NEG).astype(np.float32)
    shared = {
        "w_ada": f(w_ada), "b_all": f(b_all), "w_ada_kv": f(w_ada_kv),
        "conv_w_in": f(conv_w_in)[0], "convw": f(convw), "conv_w_out": f(conv_w_out)[0],
        "w_k": f(w_k), "w_v": f(w_v), "w_f": f(w_f), "b_f": f(b_f).reshape(NH, 1),
        "w_q": f(attn_w_q)[0], "w_o": f(attn_w_o)[0],
        "w_gate": f(ffn_w_gate), "w_up": f(ffn_w_up), "w_down": f(ffn_w_down),
        "lnv": f(lnv), "ident": ident, "trimask": trimask,
    }
    in_maps = []
    for b in range(B):
        m = dict(shared)
        m["x"] = x[b]
        m["cT"] = f(c[b].reshape(8, 128).T)
        in_maps.append(m)
    nc = _get_nc()
    res = run_bass_kernel_spmd(nc, in_maps, core_ids=list(range(B)))
    return np.stack([np.asarray(r["out"], dtype=np.float32).reshape(S, D) for r in res.results], axis=0)
```

```python
import numpy as np
from contextlib import ExitStack
import concourse.bass as bass
import concourse.mybir as mybir
from concourse.bass_utils import run_bass_kernel_spmd

F32 = mybir.dt.float32
BF16 = mybir.dt.bfloat16
AF = mybir.ActivationFunctionType
ALU = mybir.AluOpType

D = 1024
S = 4096
NH = 16
DFF = 2816
NFC = DFF // 128
TT = 512
NT = S // TT
NTR = NT
SKIP = set()
KC = D // 128
ALPHA = float((2.0 * 2) ** 0.25)
EPS = 1e-5
NEG = -30000.0


class Sm:
    def __init__(self, nc, name):
        self.h = nc.alloc_semaphore(name)
        self.n = 0
        self.tag = None


class Eng:
    def __init__(self, nc, e, name):
        self.e = e
        self.sem = Sm(nc, "es_" + name)
        self.waited = {}
        self.name = name

    def wait(self, sm, val):
        if val <= 0 or self.waited.get(sm, 0) >= val:
            return
        self.e.wait_ge(sm.h, val)
        self.waited[sm] = val


class Res:
    __slots__ = ("w", "rs")

    def __init__(self):
        self.w = {}
        self.rs = {}


def build(upto="all"):
    nc = bass.Bass("TRN2", target_bir_lowering=False)

    def din(name, shape, dt=F32):
        return nc.dram_tensor(name, shape, dt, kind="ExternalInput").ap()

    x = din("x", [S, D])
    cT = din("cT", [128, 8])
    w_ada = din("w_ada", [2, D, 6 * D])
    b_all = din("b_all", [1, 14336])
    w_ada_kv = din("w_ada_kv", [D, 2 * D])
    conv_w_in = din("conv_w_in", [D, 3 * D])
    convw = din("convw", [128, 24])
    conv_w_out = din("conv_w_out", [D, D])
    w_k = din("w_k", [D, D])
    w_v = din("w_v", [D, D])
    w_f = din("w_f", [D, NH])
    b_f = din("b_f", [NH, 1])
    w_q = din("w_q", [D, D])
    w_o = din("w_o", [D, D])
    w_gate = din("w_gate", [2, D, DFF])
    w_up = din("w_up", [2, D, DFF])
    w_down = din("w_down", [2, DFF, D])
    lnv = din("lnv", [128, 64])
    ident_in = din("ident", [128, 128])
    mask_in = din("trimask", [128, 512])
    out = nc.dram_tensor("out", [S, D], F32, kind="ExternalOutput").ap()

    modscr = nc.dram_tensor("modscr", [14336], F32).ap()
    AXS = [nc.dram_tensor("axs%d" % i, [NT, 128, KC * TT], F32).ap() for i in range(3)]
    HS = [nc.dram_tensor("hs%d" % i, [NT, 128, KC * TT], BF16).ap() for i in range(2)]
    KSd = nc.dram_tensor("ksd", [NH, 64, S], BF16).ap()
    VSd = nc.dram_tensor("vsd", [NH, 128, S // 128, 128], BF16).ap()
    AXSres = [[Res() for _ in range(NT)] for _ in range(3)]
    HSres = [[Res() for _ in range(NT)] for _ in range(2)]
    KSres = [Res() for _ in range(NT)]
    VSres = [Res() for _ in range(NT)]

    PE = Eng(nc, nc.tensor, "pe")
    ACT = Eng(nc, nc.scalar, "act")
    DVE = Eng(nc, nc.vector, "dve")
    POOL = Eng(nc, nc.gpsimd, "pool")
    SP = Eng(nc, nc.sync, "sp")

    def op(eng, fn, reads=(), writes=()):
        own = eng.sem
        for r in reads:
            for sm, v in r.w.items():
                if not (eng is PE and sm is own):
                    eng.wait(sm, v)
        for r in writes:
            for sm, v in r.w.items():
                if not (eng is PE and sm is own):
                    eng.wait(sm, v)
            for sm, v in r.rs.items():
                if not (eng is PE and sm is own):
                    eng.wait(sm, v)
        ins = fn()
        own.n += 1
        ins.then_inc(own.h, 1)
        for r in reads:
            r.rs[own] = own.n
        for r in writes:
            r.w[own] = own.n
            r.rs = {}
        return ins

    def pe_group(fns, reads, writes):
        def run():
            ins = None
            for f in fns:
                ins = f()
            return ins
        return op(PE, run, reads, writes)

    dma_sems = {}

    def dsem(name):
        if name not in dma_sems:
            dma_sems[name] = Sm(nc, "dq_" + name)
        return dma_sems[name]

    def dma(semname, tag, out_ap, in_ap, reads=(), writes=(), q=None, **kw):
        q = q or SP
        sm = dsem(semname)
        for r in reads:
            for s_, v in r.w.items():
                q.wait(s_, v)
        for r in writes:
            for s_, v in r.w.items():
                q.wait(s_, v)
            for s_, v in r.rs.items():
                q.wait(s_, v)
        if sm.tag is not None and sm.tag != tag:
            q.wait(sm, sm.n)
        sm.tag = tag
        pairs = out_ap if isinstance(out_ap, list) else [(out_ap, in_ap)]
        for o_, i_ in pairs:
            ins = q.e.dma_start(out=o_, in_=i_, **kw)
            ins.then_inc(sm.h, 16)
            sm.n += 16
        for r in reads:
            r.rs[sm] = sm.n
        for r in writes:
            r.w[sm] = sm.n
            r.rs = {}

    uid = [0]

    def SBT(name, shape, dt):
        uid[0] += 1
        return nc.sbuf_tensor("%s_%d" % (name, uid[0]), shape, dt)

    def barrier():
        engs = [PE, ACT, DVE, POOL, SP]
        sems = [e.sem for e in engs if e is not SP] + list(dma_sems.values())
        for e in engs:
            for sm in sems:
                if sm is not e.sem or e is not PE:
                    e.wait(sm, sm.n)

    banks = [nc.alloc_psum_tensor("bank%d" % i, [128, 512], F32) for i in range(8)]
    bank_res = [Res() for _ in range(8)]
    ring = {"i": 0}

    held = set()

    def nbank(hold=False):
        i = ring["i"]
        while i in held:
            i = (i + 1) % 8
        ring["i"] = (i + 1) % 8
        if hold:
            held.add(i)
        return banks[i], bank_res[i]

    def release(bank):
        held.discard(banks.index(bank))

    A = nc.alloc_sbuf_tensor
    ident = A("ident_sb", [128, 128], F32)
    identb = A("identb", [128, 128], BF16)
    onesb = A("onesb", [128, 128], BF16)
    maskf = A("maskf", [128, 512], F32)
    maskb = A("maskb", [128, 512], BF16)
    modT = A("modT", [128, 112], F32)
    lnvt = A("lnvt", [128, 64], F32)
    cvw = A("cvw", [128, 24], F32)
    cin = A("cin", [128, 8], F32)
    cact = A("cact", [128, 8], F32)
    nbf = A("nbf", [NH, 1], F32)
    NV = 16
    pv = A("pv", [128, NV * 8], F32)
    Rconst = Res()
    Rmod = Res()
    Rpv = Res()
    op(DVE, lambda: nc.vector.memset(pv[:], 0.0), writes=[Rpv])

    dma("small", "c0", ident[:], ident_in, writes=[Rconst])
    dma("small", "c0", maskf[:], mask_in, writes=[Rconst])
    dma("small", "c0", lnvt[:], lnv, writes=[Rconst])
    dma("small", "c0", cvw[:], convw, writes=[Rconst])
    dma("small", "c0", cin[:], cT, writes=[Rconst])
    dma("small", "c0", nbf[:], b_f, writes=[Rconst])
    Rc2 = Res()
    op(DVE, lambda: nc.vector.tensor_copy(out=identb[:], in_=ident[:]), reads=[Rconst], writes=[Rc2])
    op(DVE, lambda: nc.vector.tensor_copy(out=maskb[:], in_=maskf[:]), reads=[Rconst], writes=[Rc2])
    op(DVE, lambda: nc.vector.memset(onesb[:], 1.0), writes=[Rc2])
    epsc = A("epsc", [128, 1], F32)
    op(DVE, lambda: nc.vector.memset(epsc[:], EPS), writes=[Rc2])
    op(DVE, lambda: nc.vector.tensor_scalar(out=nbf[:], in0=nbf[:], scalar1=-1.0, scalar2=None, op0=ALU.mult),
       reads=[Rconst], writes=[Rc2])
    op(ACT, lambda: nc.scalar.activation(out=cact[:], in_=cin[:], func=AF.Silu), reads=[Rconst], writes=[Rc2])

    def mvec(l, v):
        return modT[:, l * 48 + v * 8: l * 48 + v * 8 + 8]

    def kvvec(v):
        return modT[:, 96 + v * 8: 96 + v * 8 + 8]

    def lvec(v, l):
        return lnvt[:, (v * 2 + l) * 8: (v * 2 + l) * 8 + 8]

    def pvec(i):
        return pv[:, i * 8: i * 8 + 8]

    def stage_mod():
        with ExitStack() as es:
            stg = [es.enter_context(SBT("mstg%d" % i, [128, 1024], F32)) for i in range(6)]
            stg_res = [Res() for _ in range(6)]
            brow = es.enter_context(SBT("brow", [1, 14336], F32))
            mrow = es.enter_context(SBT("mrow", [1, 14336], F32))
            Rb = Res()
            Rm = Res()
            dma("small", "brow", brow[:], b_all, writes=[Rb])
            cnt = 0
            for g in range(7):
                bk = [nbank() for _ in range(4)]
                for kc in range(KC):
                    for half in range(2):
                        si = cnt % 6
                        cnt += 1
                        if g < 6:
                            c_lo = (g % 3) * 2048 + half * 1024
                            src = w_ada[g // 3, kc * 128:(kc + 1) * 128, c_lo:c_lo + 1024]
                        else:
                            src = w_ada_kv[kc * 128:(kc + 1) * 128, half * 1024:(half + 1) * 1024]
                        dma("wstg%d" % si, "m%d" % cnt, stg[si][:], src, writes=[stg_res[si]],
                            q=(SP if cnt % 2 == 0 else POOL))
                        fns = []
                        for b2 in range(2):
                            b = half * 2 + b2
                            fns.append(lambda b=b, b2=b2, kc=kc, si=si: nc.tensor.matmul(
                                bk[b][0][0:1, :], cact[:, kc:kc + 1], stg[si][:, b2 * 512:(b2 + 1) * 512],
                                start=(kc == 0), stop=(kc == KC - 1)))
                        pe_group(fns, reads=[stg_res[si], Rc2], writes=[bk[half * 2][1], bk[half * 2 + 1][1]])
                for b in range(4):
                    c0 = g * 2048 + b * 512
                    op(DVE, lambda b=b, c0=c0: nc.vector.tensor_tensor(
                        out=mrow[0:1, c0:c0 + 512], in0=bk[b][0][0:1, :], in1=brow[0:1, c0:c0 + 512], op=ALU.add),
                       reads=[bk[b][1], Rb], writes=[Rm])
            Rscr = Res()
            dma("small", "mscr", modscr.rearrange("(o n) -> o n", o=1), mrow[:], reads=[Rm], writes=[Rscr])
            dma("small", "mscr2", modT[:], modscr.rearrange("(c p) -> p c", p=128), reads=[Rscr], writes=[Rmod],
                allow_slow_non_contiguous=True)
        V = nc.vector
        def dv(fn):
            op(DVE, fn, reads=[Rmod, Rconst, Rpv], writes=[Rpv])
        dv(lambda: V.tensor_scalar(out=pvec(0), in0=mvec(0, 1), scalar1=1.0, scalar2=None, op0=ALU.add))
        for l in range(2):
            b = 1 + l * 5
            dv(lambda l=l, b=b: V.tensor_scalar(out=pvec(b), in0=lvec(0, l), scalar1=ALPHA, scalar2=None, op0=ALU.mult))
            dv(lambda l=l, b=b: V.tensor_scalar(out=pvec(b + 1), in0=lvec(1, l), scalar1=ALPHA, scalar2=None, op0=ALU.mult))
            dv(lambda l=l, b=b: V.tensor_scalar(out=pvec(b + 4), in0=mvec(l, 4), scalar1=1.0, scalar2=None, op0=ALU.add))
            dv(lambda l=l, b=b: V.tensor_tensor(out=pvec(b + 2), in0=lvec(0, l), in1=pvec(b + 4), op=ALU.mult))
            dv(lambda l=l, b=b: V.tensor_tensor(out=pvec(b + 3), in0=lvec(1, l), in1=pvec(b + 4), op=ALU.mult))
            dv(lambda l=l, b=b: V.tensor_tensor(out=pvec(b + 3), in0=pvec(b + 3), in1=mvec(l, 3), op=ALU.add))
        dv(lambda: V.tensor_scalar(out=pvec(11), in0=lvec(2, 0), scalar1=ALPHA, scalar2=None, op0=ALU.mult))
        dv(lambda: V.tensor_scalar(out=pvec(12), in0=lvec(3, 0), scalar1=ALPHA, scalar2=None, op0=ALU.mult))
        dv(lambda: V.tensor_scalar(out=pvec(13), in0=mvec(1, 1), scalar1=1.0, scalar2=1.0 / ALPHA, op0=ALU.add, op1=ALU.mult))
        dv(lambda: V.tensor_scalar(out=pvec(14), in0=kvvec(1), scalar1=1.0, scalar2=1.0 / ALPHA, op0=ALU.add, op1=ALU.mult))

    RV = [Rpv, Rmod, Rconst, Rc2]

    def load_w(es_stg, dst, dst_res, src, K, N, state):
        stg, stg_res = es_stg
        ns = len(stg)
        for kc in range(K // 128):
            eng = [DVE, ACT][state[0] % 2]
            state[0] += 1
            for n0 in range(0, N, 1024):
                n1 = min(N, n0 + 1024)
                si = state[1] % ns
                state[1] += 1
                dma("wstg%d" % si, "w%d" % state[1], stg[si][:, 0:n1 - n0], src[kc * 128:(kc + 1) * 128, n0:n1],
                    writes=[stg_res[si]], q=(SP if state[1] % 2 == 0 else POOL))
                if eng is DVE:
                    op(DVE, lambda si=si, kc=kc, n0=n0, n1=n1: nc.vector.tensor_copy(out=dst[:, kc, n0:n1], in_=stg[si][:, 0:n1 - n0]),
                       reads=[stg_res[si]], writes=[dst_res[kc]])
                else:
                    op(ACT, lambda si=si, kc=kc, n0=n0, n1=n1: nc.scalar.activation(out=dst[:, kc, n0:n1], in_=stg[si][:, 0:n1 - n0],
                                                                                   func=AF.Copy),
                       reads=[stg_res[si]], writes=[dst_res[kc]])

    wstate = [0, 0]

    def ln_steps(AX, AXres, RB, RBres, RS, RSres, aG, aB, hA=None, hB=None, H=None, Hres=None, dve_out=False):
        V = nc.vector
        st = {}
        steps = []

        def s_stats1():
            st["mb"], st["mr"] = nbank(hold=True)
            mb = st["mb"]
            pe_group([lambda m=m: nc.tensor.matmul(mb[:, :], onesb[:, :], RB[:, m, :], start=(m == 0), stop=(m == KC - 1))
                      for m in range(KC)], reads=[RBres[m] for m in range(KC)] + [Rc2], writes=[st["mr"]])
        steps.append(s_stats1)

        def s_center(m):
            mb, mr = st["mb"], st["mr"]
            op(DVE, lambda: V.scalar_tensor_tensor(out=AX[:, m, :], in0=mb[:, :], scalar=-1.0 / D, in1=AX[:, m, :],
                                                   op0=ALU.mult, op1=ALU.add), reads=[mr], writes=[AXres[m]])
            op(DVE, lambda: V.tensor_tensor(out=RB[:, m, :], in0=AX[:, m, :], in1=AX[:, m, :], op=ALU.mult),
               reads=[AXres[m]], writes=[RBres[m]])
            if m == KC - 1:
                release(mb)
        for m in range(KC):
            steps.append(lambda m=m: s_center(m))

        def s_stats2():
            vb, vr = nbank()
            pe_group([lambda m=m: nc.tensor.matmul(vb[:, :], onesb[:, :], RB[:, m, :], start=(m == 0), stop=(m == KC - 1))
                      for m in range(KC)], reads=[RBres[m] for m in range(KC)] + [Rc2], writes=[vr])
            op(ACT, lambda: nc.scalar.activation(out=RS[:, :], in_=vb[:, :], func=AF.Ln, scale=1.0 / D, bias=epsc[:, 0:1]),
               reads=[vr, Rc2], writes=[RSres])
            op(ACT, lambda: nc.scalar.activation(out=RS[:, :], in_=RS[:, :], func=AF.Exp, scale=-0.5),
               reads=[RSres], writes=[RSres])
        steps.append(s_stats2)

        def s_out(m):
            op(DVE, lambda: V.tensor_tensor(out=AX[:, m, :], in0=AX[:, m, :], in1=RS[:, :], op=ALU.mult),
               reads=[RSres], writes=[AXres[m]])
            if H is not None:
                if dve_out:
                    op(DVE, lambda: V.tensor_scalar(out=H[:, m, :], in0=AX[:, m, :], scalar1=hA[:, m:m + 1],
                                                    scalar2=hB[:, m:m + 1], op0=ALU.mult, op1=ALU.add),
                       reads=[AXres[m]] + RV, writes=[Hres[m]])
                else:
                    op(ACT, lambda: nc.scalar.activation(out=H[:, m, :], in_=AX[:, m, :], func=AF.Identity,
                                                         scale=hA[:, m:m + 1], bias=hB[:, m:m + 1]),
                       reads=[AXres[m]] + RV, writes=[Hres[m]])
            if dve_out:
                op(DVE, lambda: V.tensor_scalar(out=AX[:, m, :], in0=AX[:, m, :], scalar1=aG[:, m:m + 1],
                                                scalar2=aB[:, m:m + 1], op0=ALU.mult, op1=ALU.add),
                   reads=RV, writes=[AXres[m]])
            else:
                op(ACT, lambda: nc.scalar.activation(out=AX[:, m, :], in_=AX[:, m, :], func=AF.Identity,
                                                     scale=aG[:, m:m + 1], bias=aB[:, m:m + 1]),
                   reads=RV, writes=[AXres[m]])
        for m in range(KC):
            steps.append(lambda m=m: s_out(m))
        return steps

    def finish_ln(*a, **kw):
        for f in ln_steps(*a, **kw):
            f()

    def resid(AX, AXres, RB, RBres, m, bank, bres, gvec, rb_dve=False):
        op(DVE, lambda: nc.vector.scalar_tensor_tensor(out=AX[:, m, :], in0=bank[:, :], scalar=gvec[:, m:m + 1],
                                                       in1=AX[:, m, :], op0=ALU.mult, op1=ALU.add),
           reads=[bres] + RV, writes=[AXres[m]])
        if rb_dve:
            op(DVE, lambda: nc.vector.tensor_copy(out=RB[:, m, :], in_=AX[:, m, :]), reads=[AXres[m]], writes=[RBres[m]])
        else:
            op(ACT, lambda: nc.scalar.activation(out=RB[:, m, :], in_=AX[:, m, :], func=AF.Copy),
               reads=[AXres[m]], writes=[RBres[m]])

    def flat(t):
        return t[:, :, :].rearrange("p a b -> p (a b)")

    def interleave(main, bg, start=1, every=1):
        for i, f in enumerate(main):
            f()
            if i >= start and bg and (i - start) % every == 0:
                bg.pop(0)()

    def stage_conv():
        with ExitStack() as es:
            Win = es.enter_context(SBT("Win", [128, KC, 3 * D], BF16))
            Wout = es.enter_context(SBT("Wout", [128, KC, D], BF16))
            Winr = [Res() for _ in range(KC)]
            Woutr = [Res() for _ in range(KC)]
            with ExitStack() as es2:
                stg = [es2.enter_context(SBT("wstg%d" % i, [128, 1024], F32)) for i in range(6)]
                sres = [Res() for _ in range(6)]
                load_w((stg, sres), Win, Winr, conv_w_in, D, 3 * D, wstate)
                load_w((stg, sres), Wout, Woutr, conv_w_out, D, D, wstate)
                barrier()
            XT = [es.enter_context(SBT("XT%d" % i, [128, 4, D], F32)) for i in range(2)]
            XTr = [Res() for _ in range(2)]
            AXs = [es.enter_context(SBT("AX%d" % i, [128, KC, TT], F32)) for i in range(2)]
            AXr = [[Res() for _ in range(KC)] for _ in range(2)]
            Hs = [es.enter_context(SBT("H%d" % i, [128, KC, TT], BF16)) for i in range(2)]
            Hr = [[Res() for _ in range(KC)] for _ in range(2)]
            H2 = es.enter_context(SBT("H2", [128, KC, TT], BF16))
            H2r = [Res() for _ in range(KC)]
            Z = es.enter_context(SBT("Z", [128, KC, TT], BF16))
            Zr = [Res() for _ in range(KC)]
            RB = es.enter_context(SBT("RB", [128, KC, TT], BF16))
            RBr = [Res() for _ in range(KC)]
            U = es.enter_context(SBT("U", [128, KC, TT + 2], F32))
            Ur = [Res() for _ in range(KC)]
            VSb = [es.enter_context(SBT("VSb%d" % i, [128, TT], F32)) for i in range(2)]
            VSr = [Res() for _ in range(2)]
            CV = [es.enter_context(SBT("CV%d" % i, [128, TT], F32)) for i in range(2)]
            CVr = [Res() for _ in range(2)]
            RS = es.enter_context(SBT("RS", [128, TT], F32))
            RSr = Res()
            T1 = es.enter_context(SBT("T1", [128, TT], F32))
            T1r = Res()
            for m in range(KC):
                if "umem" in SKIP:
                    break
                op(POOL, lambda m=m: nc.gpsimd.memset(U[:, m, 0:2], 0.0), writes=[Ur[m]])

            def load_x(t):
                dma("xin%d" % (t % 2), "x%d" % t, XT[t % 2][:, :, :],
                    x[t * TT:(t + 1) * TT, :].rearrange("(s p) d -> p s d", p=128), writes=[XTr[t % 2]])
            if "loadx" not in SKIP:
                load_x(0)
            A0 = pvec(0)
            B0 = mvec(0, 0)
            g1 = mvec(0, 2)
            vstate = [0]

            def front_steps(t):
                xt = XT[t % 2]
                ax, axr = AXs[t % 2], AXr[t % 2]
                h, hr = Hs[t % 2], Hr[t % 2]
                steps = []

                def sA(ch):
                    if ch == 0 and t + 1 < NTR:
                        load_x(t + 1)
                    bk, br = nbank()
                    pe_group([lambda s_=s_: nc.tensor.transpose(bk[:, s_ * 128:(s_ + 1) * 128],
                                                                xt[:, s_, ch * 128:(ch + 1) * 128], ident[:, :])
                              for s_ in range(4)], reads=[XTr[t % 2], Rconst], writes=[br])
                    op(ACT, lambda: nc.scalar.activation(out=ax[:, ch, :], in_=bk[:, :], func=AF.Copy, scale=ALPHA),
                       reads=[br], writes=[axr[ch]])
                    op(ACT, lambda: nc.scalar.activation(out=h[:, ch, :], in_=bk[:, :], func=AF.Identity,
                                                         scale=A0[:, ch:ch + 1], bias=B0[:, ch:ch + 1]),
                       reads=[br] + RV, writes=[hr[ch]])
                for ch in range(KC):
                    steps.append(lambda ch=ch: sA(ch))

                def sB(m):
                    pb = []
                    for part in range(3):
                        bk, br = nbank()
                        c0 = part * D + m * 128
                        pe_group([lambda kc=kc, c0=c0, bk=bk: nc.tensor.matmul(bk[:, :], Win[:, kc, c0:c0 + 128], h[:, kc, :],
                                                                               start=(kc == 0), stop=(kc == KC - 1))
                                  for kc in range(KC)], reads=Winr + hr, writes=[br])
                        pb.append((bk, br))
                    (bB, rB), (bC, rC), (bV, rV) = pb
                    vs, vsr = VSb[vstate[0] % 2], VSr[vstate[0] % 2]
                    cv, cvr = CV[vstate[0] % 2], CVr[vstate[0] % 2]
                    vstate[0] += 1
                    VV = nc.vector
                    op(ACT, lambda: nc.scalar.activation(out=vs[:, :], in_=bV[:, :], func=AF.Copy), reads=[rV], writes=[vsr])
                    op(DVE, lambda: VV.tensor_tensor(out=U[:, m, 2:TT + 2], in0=bC[:, :], in1=vs[:, :], op=ALU.mult),
                       reads=[rC, vsr], writes=[Ur[m]])
                    op(DVE, lambda: VV.tensor_scalar(out=cv[:, :], in0=U[:, m, 2:TT + 2], scalar1=cvw[:, 16 + m:17 + m],
                                                     scalar2=None, op0=ALU.mult), reads=[Ur[m]] + RV, writes=[cvr])
                    op(DVE, lambda: VV.scalar_tensor_tensor(out=cv[:, :], in0=U[:, m, 1:TT + 1], scalar=cvw[:, 8 + m:9 + m],
                                                            in1=cv[:, :], op0=ALU.mult, op1=ALU.add),
                       reads=[Ur[m]] + RV, writes=[cvr])
                    op(DVE, lambda: VV.scalar_tensor_tensor(out=cv[:, :], in0=U[:, m, 0:TT], scalar=cvw[:, m:m + 1],
                                                            in1=cv[:, :], op0=ALU.mult, op1=ALU.add),
                       reads=[Ur[m]] + RV, writes=[cvr])
                    op(DVE, lambda: VV.tensor_copy(out=U[:, m, 0:2], in_=U[:, m, TT:TT + 2]), writes=[Ur[m]])
                    op(DVE, lambda: VV.tensor_tensor(out=Z[:, m, :], in0=bB[:, :], in1=cv[:, :], op=ALU.mult),
                       reads=[rB, cvr], writes=[Zr[m]])
                for m in range(KC):
                    steps.append(lambda m=m: sB(m))

                def sC(m):
                    bk, br = nbank()
                    pe_group([lambda kc=kc: nc.tensor.matmul(bk[:, :], Wout[:, kc, m * 128:(m + 1) * 128], Z[:, kc, :],
                                                             start=(kc == 0), stop=(kc == KC - 1))
                              for kc in range(KC)], reads=Woutr + Zr, writes=[br])
                    resid(ax, axr, RB, RBr, m, bk, br, g1)
                for m in range(KC):
                    steps.append(lambda m=m: sC(m))
                return steps

            def back_steps(t):
                ax, axr = AXs[t % 2], AXr[t % 2]
                steps = ln_steps(ax, axr, RB, RBr, RS, RSr, pvec(1), pvec(2), pvec(3), pvec(4), H2, H2r)

                def s_out():
                    dma("aout%d" % (t % 2), "a1_%d" % t, AXS[0][t], flat(ax), reads=axr, writes=[AXSres[0][t]])
                    dma("hout%d" % (t % 2), "h2_%d" % t, HS[0][t], flat(H2), reads=H2r, writes=[HSres[0][t]])
                steps.append(s_out)
                return steps

            bg = []
            for t in range(NTR):
                interleave(front_steps(t), bg, start=1)
                while bg:
                    bg.pop(0)()
                bg = back_steps(t)
            while bg:
                bg.pop(0)()

    def stage_ffn(l, ax_in, axr_in, h_in, hr_in, ax_out, axr_out, gvec, aG, aB, final):
        with ExitStack() as es:
            Wg = es.enter_context(SBT("Wg", [128, KC, DFF], BF16))
            Wu = es.enter_context(SBT("Wu", [128, KC, DFF], BF16))
            Wd = es.enter_context(SBT("Wd", [128, NFC, D], BF16))
            CGS = [(0, 1024), (1024, 2048), (2048, DFF)]
            Wgr = [[Res() for _ in CGS] for _ in range(KC)]
            Wur = [[Res() for _ in CGS] for _ in range(KC)]
            Wdr = [Res() for _ in range(NFC)]
            AX = es.enter_context(SBT("AX", [128, KC, TT], F32))
            AXr = [Res() for _ in range(KC)]
            stg = [AX[:, 2 * i:2 * i + 2, :].rearrange("p a b -> p (a b)") for i in range(4)]
            sres = [Res() for _ in range(4)]
            pieces = []
            for cg, (n0, n1) in enumerate(CGS):
                for W_, Wr_, src_ in ((Wg, Wgr, w_gate[l]), (Wu, Wur, w_up[l])):
                    for kc in range(KC):
                        pieces.append((W_, Wr_[kc][cg], src_[kc * 128:(kc + 1) * 128, n0:n1], kc, n0, n1))
            for f in range(NFC):
                pieces.append((Wd, Wdr[f], w_down[l][f * 128:(f + 1) * 128, :], f, 0, D))
            pstate = {"i": 0}

            def emit_pieces(upto):
                while pstate["i"] < min(upto, len(pieces)):
                    i = pstate["i"]
                    pstate["i"] += 1
                    dst, dres, src, kc, n0, n1 = pieces[i]
                    si = i % 4
                    dma("wstg%d" % si, "fw%d_%d" % (l, i), stg[si][:, 0:n1 - n0], src, writes=[sres[si]],
                        q=(POOL if i % 2 == 0 else SP))
                    if i % 2 == 0:
                        op(DVE, lambda: nc.vector.tensor_copy(out=dst[:, kc, n0:n1], in_=stg[si][:, 0:n1 - n0]),
                           reads=[sres[si]], writes=[dres])
                    else:
                        op(ACT, lambda: nc.scalar.activation(out=dst[:, kc, n0:n1], in_=stg[si][:, 0:n1 - n0], func=AF.Copy),
                           reads=[sres[si]], writes=[dres])
            emit_pieces(16)
            Hs = [es.enter_context(SBT("H%d" % i, [128, KC, TT], BF16)) for i in range(2)]
            Hr = [[Res() for _ in range(KC)] for _ in range(2)]
            Gt = es.enter_context(SBT("G", [128, NFC, TT], BF16))
            Gr = [Res() for _ in range(NFC)]
            SL = [es.enter_context(SBT("SL%d" % i, [128, TT], F32)) for i in range(2)]
            SLr = [Res() for _ in range(2)]
            RS = es.enter_context(SBT("RS", [128, TT], F32))
            RSr = Res()
            if final:
                OS = [es.enter_context(SBT("OS%d" % i, [128, 512], F32)) for i in range(2)]
                OSr = [Res() for _ in range(2)]

            def load_h(t):
                dma("hin%d" % (t % 2), "hin%d_%d" % (l, t), flat(Hs[t % 2]), h_in[t], reads=[hr_in[t]], writes=Hr[t % 2])
            load_h(0)
            cnts = {"si": 0, "oi": 0}

            def front_steps(t):
                h, hr = Hs[t % 2], Hr[t % 2]
                steps = []

                def sF(f):
                    cg = f // 8
                    if t == 0:
                        emit_pieces(16 * (cg + 1))
                    bg_, rg = nbank()
                    bu, ru = nbank()
                    pe_group([lambda kc=kc: nc.tensor.matmul(bg_[:, :], Wg[:, kc, f * 128:(f + 1) * 128], h[:, kc, :],
                                                             start=(kc == 0), stop=(kc == KC - 1)) for kc in range(KC)],
                             reads=[Wgr[kc][cg] for kc in range(KC)] + hr, writes=[rg])
                    pe_group([lambda kc=kc: nc.tensor.matmul(bu[:, :], Wu[:, kc, f * 128:(f + 1) * 128], h[:, kc, :],
                                                             start=(kc == 0), stop=(kc == KC - 1)) for kc in range(KC)],
                             reads=[Wur[kc][cg] for kc in range(KC)] + hr, writes=[ru])
                    if t == 0:
                        emit_pieces(pstate["i"] + 3)
                    sl, slr = SL[cnts["si"] % 2], SLr[cnts["si"] % 2]
                    cnts["si"] += 1
                    op(ACT, lambda: nc.scalar.activation(out=sl[:, :], in_=bg_[:, :], func=AF.Silu), reads=[rg], writes=[slr])
                    op(DVE, lambda: nc.vector.tensor_tensor(out=Gt[:, f, :], in0=bu[:, :], in1=sl[:, :], op=ALU.mult),
                       reads=[ru, slr], writes=[Gr[f]])
                for f in range(NFC):
                    steps.append(lambda f=f: sF(f))
                return steps

            def down_steps(t):
                h, hr = Hs[t % 2], Hr[t % 2]
                steps = []

                def sD(m):
                    if m == 0:
                        if t == 0:
                            emit_pieces(len(pieces))
                        dma("ain", "ain%d_%d" % (l, t), flat(AX), ax_in[t], reads=[axr_in[t]],
                            writes=AXr + (sres if t == 0 else []))
                    bk, br = nbank()
                    pe_group([lambda f=f: nc.tensor.matmul(bk[:, :], Wd[:, f, m * 128:(m + 1) * 128], Gt[:, f, :],
                                                           start=(f == 0), stop=(f == NFC - 1)) for f in range(NFC)],
                             reads=Wdr + Gr, writes=[br])
                    resid(AX, AXr, h, hr, m, bk, br, gvec)
                for m in range(KC):
                    steps.append(lambda m=m: sD(m))
                return steps

            def back_steps(t):
                h, hr = Hs[t % 2], Hr[t % 2]
                steps = ln_steps(AX, AXr, h, hr, RS, RSr, aG, aB)
                if not final:
                    def s_out():
                        dma("aout%d" % (t % 2), "ao%d_%d" % (l, t), ax_out[t], flat(AX), reads=AXr, writes=[axr_out[t]])
                        if t + 2 < NTR:
                            load_h(t + 2)
                    steps.append(s_out)
                else:
                    def s_tr(s_, half):
                        bk, br = nbank()
                        pe_group([lambda q=q: nc.tensor.transpose(
                            bk[:, q * 128:(q + 1) * 128], AX[:, half * 4 + q, s_ * 128:(s_ + 1) * 128], ident[:, :])
                            for q in range(4)], reads=AXr + [Rconst], writes=[br])
                        oi = cnts["oi"]
                        cnts["oi"] += 1
                        o, orr = OS[oi % 2], OSr[oi % 2]
                        if oi % 2 == 0:
                            op(ACT, lambda: nc.scalar.activation(out=o[:, :], in_=bk[:, :], func=AF.Copy),
                               reads=[br], writes=[orr])
                        else:
                            op(DVE, lambda: nc.vector.tensor_copy(out=o[:, :], in_=bk[:, :]), reads=[br], writes=[orr])
                        r0 = t * TT + s_ * 128
                        dma("oout%d" % (oi % 2), "o%d" % oi, out[r0:r0 + 128, half * 512:(half + 1) * 512], o[:, :],
                            reads=[orr])
                        if s_ == 3 and half == 1 and t + 2 < NTR:
                            load_h(t + 2)
                    for s_ in range(4):
                        for half in range(2):
                            steps.append(lambda s_=s_, half=half: s_tr(s_, half))
                return steps

            if NTR > 1:
                load_h(1)
            bg = []
            for t in range(NTR):
                interleave(front_steps(t), bg, start=1)
                while bg:
                    bg.pop(0)()
                for f_ in down_steps(t):
                    f_()
                bg = back_steps(t)
            while bg:
                bg.pop(0)()

    NFT = A("NFT", [128, (S // 128) * NH], F32)
    NFTr = Res()
    FBd = nc.dram_tensor("fbd", [NH, S], BF16).ap()
    FBdr = [Res() for _ in range(NT)]
    HQd = nc.dram_tensor("hqd", [NT, 128, KC * TT], BF16).ap()
    HQdr = [Res() for _ in range(NT)]

    def stage_kv(pre=None):
        with ExitStack() as es:
            sbt = lambda *a: es.enter_context(SBT(*a))
            Wk = sbt("Wk", [128, KC, D], BF16)
            Wv = sbt("Wv", [128, KC, D], BF16)
            Wf = sbt("Wf", [128, KC, NH], BF16)
            Wff = sbt("Wff", [128, KC, NH], F32)
            Wkr, Wvr = ([Res() for _ in range(KC)] for _ in range(2))
            Wfr = Res()
            with ExitStack() as es2:
                stg = [es2.enter_context(SBT("wstg%d" % i, [128, 1024], F32)) for i in range(6)]
                sres = [Res() for _ in range(6)]
                load_w((stg, sres), Wk, Wkr, w_k, D, D, wstate)
                load_w((stg, sres), Wv, Wvr, w_v, D, D, wstate)
                barrier()
            Rwff = Res()
            dma("small", "wf", Wff[:, :, :], w_f.rearrange("(kc p) n -> p kc n", p=128), writes=[Rwff])
            op(DVE, lambda: nc.vector.tensor_copy(out=Wf[:, :, :], in_=Wff[:, :, :]), reads=[Rwff], writes=[Wfr])
            AXs = [sbt("AXk%d" % i, [128, KC, TT], F32) for i in range(2)]
            AXr = [[Res() for _ in range(KC)] for _ in range(2)]
            HKVs = [sbt("HKV%d" % i, [128, KC, TT], BF16) for i in range(2)]
            HKVr = [[Res() for _ in range(KC)] for _ in range(2)]
            KTs = [sbt("KT%d" % i, [128, KC, TT], BF16) for i in range(2)]
            KTr = [Res() for _ in range(2)]
            VST = [sbt("VST%d" % i, [128, NH, 128], BF16) for i in range(3)]
            VSTr = [Res() for _ in range(3)]
            LF = sbt("LF", [NH, TT], F32)
            LFr = Res()
            NF = sbt("NF", [NH, TT], F32)
            NFr = Res()
            ONE16 = sbt("ONE16", [NH, TT], F32)
            CAR = sbt("CAR", [NH, 1], F32)
            CARr = Res()
            FB = sbt("FB", [NH, TT], BF16)
            FBr = Res()
            V = nc.vector
            op(DVE, lambda: V.memset(ONE16[:, :], 1.0), writes=[CARr])
            op(DVE, lambda: V.memset(CAR[:, :], 0.0), writes=[CARr])
            for i in range(3):
                op(POOL, lambda i=i: nc.gpsimd.memset(VST[i][:, :, 64:128], 1.0), writes=[VSTr[i]])
            kvA, kvB = pvec(14), kvvec(0)
            KS4 = KSd.rearrange("(pr a) d t -> a d pr t", a=2)
            VS4 = VSd.rearrange("h p j c -> p h j c")
            vcnt = 0
            ecnt = 0

            def load_ax(t):
                dma("ain%d" % (t % 2), "aink_%d" % t, flat(AXs[t % 2]), AXS[1][t], reads=[AXSres[1][t]], writes=AXr[t % 2])
            HQo = [sbt("HQo%d" % i, [128, KC, TT], BF16) for i in range(2)]
            HQor = [[Res() for _ in range(KC)] for _ in range(2)]
            qA_, qB_ = pvec(13), mvec(1, 0)
            pstg = [sbt("pstg%d" % i, [128, 1024], F32) for i in range(6)]
            psres = [Res() for _ in range(6)]
            ppieces = []
            if pre is not None:
                for W_, Wr_, src_ in ((pre[0], pre[2], w_q), (pre[1], pre[3], w_o)):
                    for kc in range(KC):
                        ppieces.append((W_, Wr_[kc], src_[kc * 128:(kc + 1) * 128, :], kc))
            pp = {"d": 0, "c": 0}

            def pre_point(last=False):
                while True:
                    progressed = False
                    if pp["d"] < len(ppieces) and pp["d"] < pp["c"] + 3:
                        i = pp["d"]
                        pp["d"] += 1
                        si = i % 6
                        dma("pstg%d" % si, "pw%d" % i, pstg[si][:, :], ppieces[i][2], writes=[psres[si]],
                            q=(POOL if i % 2 == 0 else SP))
                        progressed = True
                    if pp["c"] < pp["d"] and (last or pp["c"] + 2 < pp["d"]):
                        i = pp["c"]
                        pp["c"] += 1
                        si = i % 6
                        dst, dres, _, kc = ppieces[i]
                        if i % 2 == 0:
                            op(DVE, lambda: V.tensor_copy(out=dst[:, kc, :], in_=pstg[si][:, :]), reads=[psres[si]], writes=[dres])
                        else:
                            op(ACT, lambda: nc.scalar.activation(out=dst[:, kc, :], in_=pstg[si][:, :], func=AF.Copy),
                               reads=[psres[si]], writes=[dres])
                        progressed = True
                    if not last or not progressed:
                        break

            def make_hkv(c):
                AX, axr = AXs[c % 2], AXr[c % 2]
                HKV, hkvr = HKVs[c % 2], HKVr[c % 2]
                hqo, hqor = HQo[c % 2], HQor[c % 2]
                for ch in range(KC):
                    if ch % 2 == 1:
                        op(ACT, lambda ch=ch: nc.scalar.activation(out=hqo[:, ch, :], in_=AX[:, ch, :], func=AF.Identity,
                                                                   scale=qA_[:, ch:ch + 1], bias=qB_[:, ch:ch + 1]),
                           reads=[axr[ch]] + RV, writes=[hqor[ch]])
                    else:
                        op(DVE, lambda ch=ch: V.tensor_scalar(out=hqo[:, ch, :], in0=AX[:, ch, :], scalar1=qA_[:, ch:ch + 1],
                                                              scalar2=qB_[:, ch:ch + 1], op0=ALU.mult, op1=ALU.add),
                           reads=[axr[ch]] + RV, writes=[hqor[ch]])
                dma("hqout%d" % (c % 2), "hqo%d" % c, HQd[c], flat(hqo), reads=hqor, writes=[HQdr[c]])
                for ch in range(KC):
                    if ch % 2 == 0:
                        op(ACT, lambda ch=ch: nc.scalar.activation(out=HKV[:, ch, :], in_=AX[:, ch, :], func=AF.Identity,
                                                                   scale=kvA[:, ch:ch + 1], bias=kvB[:, ch:ch + 1]),
                           reads=[axr[ch]] + RV, writes=[hkvr[ch]])
                    else:
                        op(DVE, lambda ch=ch: V.tensor_scalar(out=HKV[:, ch, :], in0=AX[:, ch, :], scalar1=kvA[:, ch:ch + 1],
                                                              scalar2=kvB[:, ch:ch + 1], op0=ALU.mult, op1=ALU.add),
                           reads=[axr[ch]] + RV, writes=[hkvr[ch]])

            load_ax(0)
            if NTR > 1:
                load_ax(1)
            make_hkv(0)
            for c in range(NTR):
                HKV, hkvr = HKVs[c % 2], HKVr[c % 2]
                KT, ktr = KTs[c % 2], KTr[c % 2]
                if c + 1 < NTR:
                    make_hkv(c + 1)
                if c + 2 < NTR:
                    load_ax(c + 2)
                bk, br = nbank()
                pe_group([lambda kc=kc: nc.tensor.matmul(bk[0:NH, :], Wf[:, kc, :], HKV[:, kc, :],
                                                         start=(kc == 0), stop=(kc == KC - 1)) for kc in range(KC)],
                         reads=[Wfr] + hkvr, writes=[br])
                op(ACT, lambda: nc.scalar.activation(out=LF[:, :], in_=bk[0:NH, :], func=AF.Exp, scale=-1.0, bias=nbf[:, 0:1]),
                   reads=[br, Rc2], writes=[LFr])
                op(ACT, lambda: nc.scalar.activation(out=LF[:, :], in_=LF[:, :], func=AF.Ln, bias=1.0), reads=[LFr], writes=[LFr])
                op(DVE, lambda: V.tensor_tensor_scan(out=NF[:, :], data0=ONE16[:, :], data1=LF[:, :], initial=CAR[:, 0:1],
                                                     op0=ALU.mult, op1=ALU.add), reads=[LFr, CARr], writes=[NFr])
                op(DVE, lambda: V.tensor_copy(out=CAR[:, 0:1], in_=NF[:, TT - 1:TT]), reads=[NFr], writes=[CARr])
                op(DVE, lambda: V.tensor_scalar(out=FB[:, :], in0=NF[:, :], scalar1=-1.0, scalar2=None, op0=ALU.mult),
                   reads=[NFr], writes=[FBr])
                dma("fbout", "fb%d" % c, FBd[:, c * TT:(c + 1) * TT], FB[:, :], reads=[FBr], writes=[FBdr[c]])
                for pr in range(KC):
                    bk, br = nbank()
                    pe_group([lambda kc=kc, bk=bk: nc.tensor.matmul(bk[:, :], Wk[:, kc, pr * 128:(pr + 1) * 128], HKV[:, kc, :],
                                                                    start=(kc == 0), stop=(kc == KC - 1)) for kc in range(KC)],
                             reads=Wkr + hkvr, writes=[br])
                    if ecnt % 2 == 0:
                        op(ACT, lambda: nc.scalar.activation(out=KT[:, pr, :], in_=bk[:, :], func=AF.Copy), reads=[br], writes=[ktr])
                    else:
                        op(DVE, lambda: V.tensor_copy(out=KT[:, pr, :], in_=bk[:, :]), reads=[br], writes=[ktr])
                    ecnt += 1
                dma("kout%d" % (c % 2), "ko%d" % c, [(KS4[a, :, :, c * TT:(c + 1) * TT], KT[a * 64:(a + 1) * 64, :, :]) for a in range(2)],
                    None, reads=[ktr], writes=[KSres[c]])
                bk2, br2 = nbank()
                pe_group([lambda s_=s_: nc.tensor.transpose(bk2[:, s_ * NH:(s_ + 1) * NH], NF[0:NH, s_ * 128:(s_ + 1) * 128],
                                                            ident[0:NH, 0:NH]) for s_ in range(4)],
                         reads=[NFr, Rconst], writes=[br2])
                op(DVE, lambda: V.tensor_copy(out=NFT[:, c * 4 * NH:(c + 1) * 4 * NH], in_=bk2[:, 0:4 * NH]),
                   reads=[br2], writes=[NFTr])
                for s_ in range(4):
                    vs_, vsr_ = VST[vcnt % 3], VSTr[vcnt % 3]
                    for half in range(2):
                        bk, br = nbank()
                        pe_group([lambda kc=kc, bk=bk: nc.tensor.matmul(bk[:, :], HKV[:, kc, s_ * 128:(s_ + 1) * 128],
                                                                        Wv[:, kc, half * 512:(half + 1) * 512],
                                                                        start=(kc == 0), stop=(kc == KC - 1)) for kc in range(KC)],
                                 reads=Wvr + hkvr, writes=[br])
                        src = bk[:, :].rearrange("p (h d) -> p h d", d=64)
                        if ecnt % 2 == 0:
                            op(ACT, lambda: nc.scalar.activation(out=vs_[:, half * 8:half * 8 + 8, 0:64], in_=src, func=AF.Copy),
                               reads=[br], writes=[vsr_])
                        else:
                            op(DVE, lambda: V.tensor_copy(out=vs_[:, half * 8:half * 8 + 8, 0:64], in_=src), reads=[br], writes=[vsr_])
                        ecnt += 1
                    dma("vout%d" % (vcnt % 3), "vo%d" % vcnt, VS4[:, :, c * 4 + s_, :], vs_[:, :, :],
                        reads=[vsr_], writes=[VSres[c]])
                    vcnt += 1
                    if s_ % 2 == 1:
                        pre_point()
                pre_point()
                if c == NTR - 1:
                    pre_point(last=True)

    def stage_attn(pre):
        with ExitStack() as es:
            sbt = lambda *a: es.enter_context(SBT(*a))
            Wq, Wo, Wqr, Wor = pre
            AX1 = sbt("AXa", [128, KC, TT], F32)
            AX1r = [Res() for _ in range(KC)]
            RBa = sbt("RBa", [128, KC, TT], BF16)
            RBar = [Res() for _ in range(KC)]
            HQs = [sbt("HQ%d" % i, [128, KC, TT], BF16) for i in range(2)]
            HQr = [[Res() for _ in range(KC)] for _ in range(2)]
            QTs = [sbt("QT%d" % i, [65, NH, TT], BF16) for i in range(2)]
            QTr = [[Res() for _ in range(NH)] for _ in range(2)]
            OTs = [sbt("OT%d" % i, [128, KC, TT], BF16) for i in range(2)]
            OTrs = [[Res() for _ in range(KC)] for _ in range(2)]
            H4 = sbt("H4", [128, KC, TT], BF16)
            H4r = [Res() for _ in range(KC)]
            KB = [sbt("KB%d" % i, [65, S], BF16) for i in range(3)]
            KBr = [Res() for _ in range(3)]
            VB = [sbt("VB%d" % i, [128, S // 128, 128], BF16) for i in range(3)]
            VBr = [Res() for _ in range(3)]
            PT = [sbt("PT%d" % i, [128, TT], BF16) for i in range(5)]
            PTr = [Res() for _ in range(5)]
            RL = sbt("RL", [128, TT], F32)
            RLr = Res()
            RS = sbt("RS", [128, TT], F32)
            RSr = Res()
            OBS = [sbt("OBS%d" % i, [128, TT], F32) for i in range(2)]
            OBSr = [Res() for _ in range(2)]
            V = nc.vector
            for i in range(3):
                op(POOL, lambda i=i: nc.gpsimd.memset(KB[i][64:65, :], 1.0), writes=[KBr[i]])
            qA, qB = pvec(13), mvec(1, 0)
            g1 = mvec(1, 2)
            hcnt = 0
            pcnt = 0

            def load_kv(c, h, slot):
                n = (c + 1) * TT
                dma("kin%d" % slot, "k%d_%d" % (c, h), KB[slot][0:64, 0:n], KSd[h, :, 0:n],
                    reads=KSres[0:c + 1], writes=[KBr[slot]])
                dma("vin%d" % slot, "v%d_%d" % (c, h), VB[slot][:, 0:4 * (c + 1), :], VSd[h, :, 0:4 * (c + 1), :],
                    reads=VSres[0:c + 1], writes=[VBr[slot]])

            def front_steps(c):
                HQ, hqr = HQs[c % 2], HQr[c % 2]
                QT, qtr = QTs[c % 2], QTr[c % 2]
                steps = []

                def s0():
                    dma("hqin%d" % (c % 2), "hqin_%d" % c, flat(HQ), HQd[c], reads=[HQdr[c]], writes=hqr)
                    dma("qrow", "qr%d" % c, [(QT[64:65, h, :], FBd[h:h + 1, c * TT:(c + 1) * TT]) for h in range(NH)], None,
                        reads=[FBdr[c]], writes=qtr)
                steps.append(s0)
                steps += [lambda: None] * 3

                def s_q(pr):
                    bk, br = nbank()
                    pe_group([lambda kc=kc: nc.tensor.matmul(bk[:, :], Wq[:, kc, pr * 128:(pr + 1) * 128], HQ[:, kc, :],
                                                             start=(kc == 0), stop=(kc == KC - 1)) for kc in range(KC)],
                             reads=Wqr + hqr, writes=[br])
                    op(DVE, lambda: V.tensor_scalar(out=QT[0:64, 2 * pr, :], in0=bk[0:64, :], scalar1=0.125, scalar2=None,
                                                    op0=ALU.mult), reads=[br], writes=[qtr[2 * pr]])
                    op(DVE, lambda: V.tensor_scalar(out=QT[0:64, 2 * pr + 1, :], in0=bk[64:128, :], scalar1=0.125, scalar2=None,
                                                    op0=ALU.mult), reads=[br], writes=[qtr[2 * pr + 1]])
                for pr in range(KC):
                    steps.append(lambda pr=pr: s_q(pr))
                return steps

            def back_steps(c):
                AX, axr = AX1, AX1r
                HQ, hqr = RBa, RBar
                OT, OTr = OTs[c % 2], OTrs[c % 2]
                steps = []

                def s_ld():
                    dma("ain", "aina_%d" % c, flat(AX), AXS[1][c], reads=[AXSres[1][c]], writes=axr)
                steps.append(s_ld)

                def s_wo(m):
                    bk, br = nbank()
                    pe_group([lambda kc=kc: nc.tensor.matmul(bk[:, :], Wo[:, kc, m * 128:(m + 1) * 128], OT[:, kc, :],
                                                             start=(kc == 0), stop=(kc == KC - 1)) for kc in range(KC)],
                             reads=Wor + OTr, writes=[br])
                    resid(AX, axr, HQ, hqr, m, bk, br, g1, rb_dve=True)
                for m in range(KC):
                    steps.append(lambda m=m: s_wo(m))
                ln = ln_steps(AX, axr, HQ, hqr, RS, RSr, pvec(6), pvec(7), pvec(8), pvec(9), H4, H4r, dve_out=True)
                sp = lambda: None
                steps += [sp] + ln[0:1] + ln[1:9] + [sp] + ln[9:10] + [sp] + ln[10:]

                def s_out():
                    dma("aout%d" % (c % 2), "a3_%d" % c, AXS[2][c], flat(AX), reads=axr, writes=[AXSres[2][c]])
                    dma("hout%d" % (c % 2), "h4_%d" % c, HS[1][c], flat(H4), reads=H4r, writes=[HSres[1][c]])
                steps.append(s_out)
                return steps

            for f in front_steps(0):
                f()
            GH = [(c_, h_) for c_ in range(NTR) for h_ in range(NH)]

            def issue(g):
                if g < len(GH):
                    load_kv(GH[g][0], GH[g][1], g % 3)
            for g in range(3):
                issue(g)
            bg = []
            for c in range(NTR):
                QT, qtr = QTs[c % 2], QTr[c % 2]
                OT, OTr = OTs[c % 2], OTrs[c % 2]
                nj = 4 * c + 4
                LOOK = 4
                hslot = {}
                hob = {}
                pend = []
                hbase = c * NH
                if c + 1 < NTR:
                    bg = front_steps(c + 1) + bg
                usable = (NH - 1) * nj - 2
                every = max(1, min(12, usable // (len(bg) + 1)))

                def qk(h, j):
                    nonlocal pcnt
                    slot = hslot[h]
                    kb = KB[slot]
                    r = max(0, j - 4 * c)
                    c0 = 128 * r
                    sb_, sr_ = nbank()
                    fns = [lambda: nc.tensor.matmul(sb_[:, c0:TT], kb[0:65, j * 128:(j + 1) * 128], QT[0:65, h, c0:TT],
                                                    start=True, stop=(j < 4 * c))]
                    if j >= 4 * c:
                        fns.append(lambda: nc.tensor.matmul(sb_[:, c0:TT], identb[:, :], maskb[:, 0:TT - c0],
                                                            start=False, stop=True))
                    pe_group(fns, reads=[KBr[slot], qtr[h], Rc2], writes=[sr_])
                    pt, ptr = PT[pcnt % 5], PTr[pcnt % 5]
                    pcnt += 1
                    col = j * NH + h
                    op(ACT, lambda: nc.scalar.activation(out=pt[:, c0:TT], in_=sb_[:, c0:TT], func=AF.Exp,
                                                         bias=NFT[:, col:col + 1], scale=1.0),
                       reads=[sr_, NFTr], writes=[ptr])
                    return (h, j, c0, pt, ptr)

                def pvm(item):
                    h, j, c0, pt, ptr = item
                    slot = hslot[h]
                    ob, obr = hob[h]
                    pe_group([lambda: nc.tensor.matmul(ob[:, c0:TT], VB[slot][:, j, :], pt[:, c0:TT],
                                                       start=(j == 0), stop=(j == nj - 1))],
                             reads=[VBr[slot], ptr], writes=[obr])
                    if j == nj - 1:
                        obs, obsr = OBS[h % 2], OBSr[h % 2]
                        op(DVE, lambda: V.tensor_copy(out=obs[:, :], in_=ob[:, :]), reads=[obr], writes=[obsr])
                        release(ob)
                        if c <= 2:
                            op(ACT, lambda: nc.scalar.activation(out=RL[0:64, :], in_=obs[64:128, :], func=AF.Ln),
                               reads=[obsr], writes=[RLr])
                            op(ACT, lambda: nc.scalar.activation(out=RL[0:64, :], in_=RL[0:64, :], func=AF.Exp, scale=-1.0),
                               reads=[RLr], writes=[RLr])
                        else:
                            op(DVE, lambda: V.reciprocal(out=RL[0:64, :], in_=obs[64:128, :]), reads=[obsr], writes=[RLr])
                        d0 = (h % 2) * 64
                        op(DVE, lambda: V.tensor_tensor(out=OT[d0:d0 + 64, h // 2, :], in0=obs[0:64, :], in1=RL[0:64, :],
                                                        op=ALU.mult), reads=[obsr, RLr], writes=[OTr[h // 2]])
                        issue(hbase + h + 3)

                it = 0
                for h in range(NH):
                    hslot[h] = (hbase + h) % 3
                    hob[h] = nbank(hold=True)
                    for j in range(nj):
                        pend.append(qk(h, j))
                        if len(pend) > LOOK:
                            pvm(pend.pop(0))
                        it += 1
                        if bg and it >= 2 and h < NH - 1 and (it - 2) % every == 0:
                            bg.pop(0)()
                while pend:
                    pvm(pend.pop(0))
                while bg:
                    bg.pop(0)()
                bg += back_steps(c)
            while bg:
                bg.pop(0)()

    order = ["mod", "conv", "ffn0", "attn", "all"]
    lvl = order.index(upto)
    stage_mod()
    barrier()
    if lvl >= 1:
        stage_conv()
        barrier()
    if lvl >= 2:
        stage_ffn(0, AXS[0], AXSres[0], HS[0], HSres[0], AXS[1], AXSres[1], mvec(0, 5), pvec(11), pvec(12), final=False)
        barrier()
    if lvl >= 3:
        with ExitStack() as es0:
            pre = (es0.enter_context(SBT("Wq", [128, KC, D], BF16)), es0.enter_context(SBT("Wo", [128, KC, D], BF16)),
                   [Res() for _ in range(KC)], [Res() for _ in range(KC)])
            stage_kv(pre)
            barrier()
            stage_attn(pre)
            barrier()
    if lvl >= 4:
        stage_ffn(1, AXS[2], AXSres[2], HS[1], HSres[1], None, None, mvec(1, 5), lvec(2, 1), lvec(3, 1), final=True)
    if upto != "all":
        dbg = nc.dram_tensor("dbg", [NT, 128, KC * TT], F32, kind="ExternalOutput").ap()
        dbg2 = nc.dram_tensor("dbg2", [128, 112 + NV * 8], F32, kind="ExternalOutput").ap()
        dma("small", "dbg2", dbg2[:, 0:112], modT[:, :], reads=[Rmod, Rpv])
        dma("small", "dbg2", dbg2[:, 112:112 + NV * 8], pv[:, :], reads=[Rmod, Rpv])
        if lvl >= 1:
            src = AXS[lvl - 1]
            for t in range(NTR):
                if "noout" in SKIP:
                    break
                dma("small", "dbg", dbg[t], src[t], reads=[AXSres[lvl - 1][t]])
    for name, sm in dma_sems.items():
        SP.wait(sm, sm.n)
    return nc


_NC = None


def _get_nc():
    global _NC
    if _NC is None:
        _NC = build()
    return _NC


def kernel(x, c, w_ada, b_ada, conv_w_in, conv_w, conv_w_out, w_ada_kv, b_ada_kv,
           w_k, w_v, w_f, b_f, attn_w_q, attn_w_o, ffn_w_gate, ffn_w_up, ffn_w_down,
           ln1_g, ln1_b, ln2_g, ln2_b):
    f = lambda a: np.ascontiguousarray(np.asarray(a), dtype=np.float32)
    x = f(x); c = f(c)
    B = x.shape[0]
    b_all = np.concatenate([f(b_ada).reshape(-1), f(b_ada_kv).reshape(-1)])[None, :]
    convw = f(conv_w)[0].reshape(3, 8, 128).transpose(2, 0, 1).reshape(128, 24)
    lnv = np.stack([f(ln1_g), f(ln1_b), f(ln2_g), f(ln2_b)]).reshape(4, 2, 8, 128).transpose(3, 0, 1, 2).reshape(128, 64)
    ident = np.eye(128, dtype=np.float32)
    kk = np.arange(128)[:, None]
    qq = np.arange(512)[None, :]
    trimask = np.where(qq >= kk, 0.0, NEG).astype(np.float32)
    shared = {
        "w_ada": f(w_ada), "b_all": f(b_all), "w_ada_kv": f(w_ada_kv),
        "conv_w_in": f(conv_w_in)[0], "convw": f(convw), "conv_w_out": f(conv_w_out)[0],
        "w_k": f(w_k), "w_v": f(w_v), "w_f": f(w_f), "b_f": f(b_f).reshape(NH, 1),
        "w_q": f(attn_w_q)[0], "w_o": f(attn_w_o)[0],
        "w_gate": f(ffn_w_gate), "w_up": f(ffn_w_up), "w_down": f(ffn_w_down),
        "lnv": f(lnv), "ident": ident, "trimask": trimask,
    }
    in_maps = []
    for b in range(B):
        m = dict(shared)
        m["x"] = x[b]
        m["cT"] = f(c[b].reshape(8, 128).T)
        in_maps.append(m)
    nc = _get_nc()
    res = run_bass_kernel_spmd(nc, in_maps, core_ids=list(range(B)))
    return np.stack([np.asarray(r["out"], dtype=np.float32).reshape(S, D) for r in res.results], axis=0)
```
